# Optimizing a Trainium2 kernel written in Bass

```python
import jax, jax.numpy as jnp
from jax import lax
import numpy as np

D_MODEL = 1024
BATCH = 4
SEQ = 8192
DEPTH = 1

GRID_W = 64
D_MIX = D_MODEL
D_CONV = D_MIX // 2
D_ATTN = D_MIX - D_CONV
HEAD_DIM = 64
N_HEADS = D_ATTN // HEAD_DIM
CONV_WIDTH = 31
NA_ROWS = 8
NA_COLS = 16
D_FF = -(-8 * D_MODEL // (3 * 256)) * 256
D_IN = 2 * D_CONV + 3 * D_ATTN
EPS = 1e-6

kernel_name = "hybrid_conformer_conv_neighbourhood_attn_block"


def rms_norm(x, g):
    xf = x.astype(jnp.float32)
    y = xf * lax.rsqrt(jnp.mean(xf * xf, axis=-1, keepdims=True) + EPS)
    return (y * g.astype(jnp.float32)).astype(x.dtype)


def layer_norm(x, g, b):
    xf = x.astype(jnp.float32)
    mu = jnp.mean(xf, axis=-1, keepdims=True)
    xc = xf - mu
    y = xc * lax.rsqrt(jnp.mean(xc * xc, axis=-1, keepdims=True) + EPS)
    return (y * g.astype(jnp.float32) + b.astype(jnp.float32)).astype(x.dtype)


def conformer_conv(a, gate, dw_kernel, dw_bias, ln_g, ln_b):
    u = a * jax.nn.sigmoid(gate)
    pad = CONV_WIDTH // 2
    y = lax.conv_general_dilated(
        u, dw_kernel[:, None, :], window_strides=(1,), padding=[(pad, pad)],
        dimension_numbers=("NWC", "WIO", "NWC"), feature_group_count=u.shape[-1])
    y = y + dw_bias
    y = layer_norm(y, ln_g, ln_b)
    return jax.nn.silu(y)


def neighbourhood_attention(q, k, v, rpb):
    B, S, H, Dh = q.shape
    rows = S // GRID_W
    kh = min(NA_ROWS, rows)
    kw = NA_COLS
    qg = q.reshape(B, rows, GRID_W, H, Dh)
    kg = k.reshape(B, rows, GRID_W, H, Dh)
    vg = v.reshape(B, rows, GRID_W, H, Dh)
    cols = jnp.arange(GRID_W)
    col_start = jnp.clip(cols - kw // 2, 0, GRID_W - kw)
    col_idx = col_start[:, None] + jnp.arange(kw)[None, :]
    col_bias_idx = col_idx - cols[:, None] + (NA_COLS - 1)
    scale = Dh ** -0.5

    def row_block(r):
        row_start = jnp.clip(r - kh // 2, 0, rows - kh)
        q_r = lax.dynamic_index_in_dim(qg, r, axis=1, keepdims=False)
        k_rows = lax.dynamic_slice_in_dim(kg, row_start, kh, axis=1)
        v_rows = lax.dynamic_slice_in_dim(vg, row_start, kh, axis=1)
        k_win = k_rows[:, :, col_idx]
        v_win = v_rows[:, :, col_idx]
        s = jnp.einsum("bqhd,brqchd->bhqrc", q_r, k_win,
                       preferred_element_type=jnp.float32) * scale
        row_bias_idx = row_start + jnp.arange(kh) - r + (NA_ROWS - 1)
        bias = rpb[:, row_bias_idx][:, :, col_bias_idx]
        s = s + jnp.transpose(bias, (0, 2, 1, 3))[None].astype(jnp.float32)
        p = jax.nn.softmax(s, axis=(-2, -1))
        return jnp.einsum("bhqrc,brqchd->bqhd", p.astype(v_win.dtype), v_win)

    out = lax.map(row_block, jnp.arange(rows))
    return jnp.moveaxis(out, 0, 1).reshape(B, S, H * Dh)


def setup_inputs(seed: int = 0) -> dict:
    key = jax.random.key(seed)
    ks = jax.random.split(key, 16)
    f32 = jnp.float32
    n = lambda k, shape, s: (jax.random.normal(k, shape, f32) * s).astype(f32)
    return {
        "x": n(ks[0], (BATCH, SEQ, D_MODEL), 1.0),
        "norm1_g": 1.0 + n(ks[1], (DEPTH, D_MODEL), 0.02),
        "w_in": n(ks[2], (DEPTH, D_MODEL, D_IN), D_MODEL ** -0.5),
        "q_norm_g": 1.0 + n(ks[3], (DEPTH, HEAD_DIM), 0.02),
        "k_norm_g": 1.0 + n(ks[4], (DEPTH, HEAD_DIM), 0.02),
        "rpb": n(ks[5], (DEPTH, N_HEADS, 2 * NA_ROWS - 1, 2 * NA_COLS - 1), 0.1),
        "dw_kernel": n(ks[6], (DEPTH, CONV_WIDTH, D_CONV), CONV_WIDTH ** -0.5),
        "dw_bias": n(ks[7], (DEPTH, D_CONV), 0.02),
        "conv_ln_g": 1.0 + n(ks[8], (DEPTH, D_CONV), 0.02),
        "conv_ln_b": n(ks[9], (DEPTH, D_CONV), 0.02),
        "w_out": n(ks[10], (DEPTH, D_MIX, D_MODEL), D_MIX ** -0.5),
        "norm2_g": 1.0 + n(ks[11], (DEPTH, D_MODEL), 0.02),
        "w_gate": n(ks[12], (DEPTH, D_MODEL, D_FF), D_MODEL ** -0.5),
        "w_up": n(ks[13], (DEPTH, D_MODEL, D_FF), D_MODEL ** -0.5),
        "w_down": n(ks[14], (DEPTH, D_FF, D_MODEL), D_FF ** -0.5),
    }


def reference(x, norm1_g, w_in, q_norm_g, k_norm_g, rpb, dw_kernel, dw_bias,
              conv_ln_g, conv_ln_b, w_out, norm2_g, w_gate, w_up, w_down):
    B, S, _ = x.shape
    splits = [D_CONV, 2 * D_CONV, 2 * D_CONV + D_ATTN, 2 * D_CONV + 2 * D_ATTN]
    for l in range(DEPTH):
        h = rms_norm(x, norm1_g[l])
        z = h @ w_in[l]
        a, gate, q, k, v = jnp.split(z, splits, axis=-1)
        conv_out = conformer_conv(a, gate, dw_kernel[l], dw_bias[l],
                                  conv_ln_g[l], conv_ln_b[l])
        q = rms_norm(q.reshape(B, S, N_HEADS, HEAD_DIM), q_norm_g[l])
        k = rms_norm(k.reshape(B, S, N_HEADS, HEAD_DIM), k_norm_g[l])
        v = v.reshape(B, S, N_HEADS, HEAD_DIM)
        attn_out = neighbourhood_attention(q, k, v, rpb[l])
        mix = jnp.concatenate([conv_out, attn_out], axis=-1)
        x = x + mix @ w_out[l]
        h = rms_norm(x, norm2_g[l])
        x = x + (jax.nn.silu(h @ w_gate[l]) * (h @ w_up[l])) @ w_down[l]
    return x
```

```python
import numpy as np
import concourse.bass as bass
import concourse.mybir as mybir
from concourse.bass_utils import run_bass_kernel_spmd
from contextlib import ExitStack, nullcontext

F32 = mybir.dt.float32
BF16 = mybir.dt.bfloat16
AF = mybir.ActivationFunctionType
ALU = mybir.AluOpType
AX = mybir.AxisListType


class _Op:
    __slots__ = ("eng", "fn", "deps", "signal", "sigcount", "dma", "sem", "semval", "raw")

    def __init__(self, eng, fn, dma, sem):
        self.eng = eng
        self.fn = fn
        self.deps = []
        self.signal = False
        self.sigcount = 0
        self.dma = dma
        self.sem = sem
        self.semval = 0
        self.raw = set()


class Sched:
    ENGS = ("pe", "act", "dve", "pool", "sp")

    def __init__(self, nc, sems):
        self.nc = nc
        self.ops = {e: [] for e in self.ENGS}
        self.res = {}
        self.all = []
        self.csem = sems
        self.dma_cnt = {}

    def add(self, eng, fn, reads=(), writes=(), dma_sem=None):
        op = _Op(eng, fn, dma_sem is not None, dma_sem)
        deps = {}
        for r in reads:
            st = self.res.get(r)
            if st is not None and st[0] is not None:
                deps[id(st[0])] = (st[0], True)
        for w in writes:
            st = self.res.get(w)
            if st is not None:
                if st[0] is not None and id(st[0]) not in deps:
                    deps[id(st[0])] = (st[0], False)
                for rd in st[1]:
                    if id(rd) not in deps:
                        deps[id(rd)] = (rd, False)
        for r in reads:
            st = self.res.get(r)
            if st is None:
                st = [None, []]
                self.res[r] = st
            st[1].append(op)
        for w in writes:
            self.res[w] = [op, []]
        for d, israw in deps.values():
            if d is op:
                continue
            if d.dma:
                op.deps.append(d)
            elif d.eng == op.eng and not op.dma:
                if op.eng != "pe":
                    op.deps.append(d)
            elif d.eng == op.eng and op.dma:
                op.deps.append(d)
            else:
                op.deps.append(d)
        if op.dma:
            c = self.dma_cnt.get(id(dma_sem), 0) + 16
            self.dma_cnt[id(dma_sem)] = c
            op.semval = c
        self.ops[eng].append(op)
        self.all.append(op)
        return op

    def emit(self, block):
        for op in self.all:
            for d in op.deps:
                if not d.dma:
                    d.signal = True
        for e in self.ENGS:
            c = 0
            for op in self.ops[e]:
                if op.signal and not op.dma:
                    c += 1
                op.sigcount = c
        sched = self

        def run(ename, eng):
            known = {}
            for op in sched.ops[ename]:
                need = {}
                for d in op.deps:
                    if d.dma:
                        key = ("d", id(d.sem))
                        v = d.semval
                        sm = d.sem
                    else:
                        key = ("c", d.eng)
                        v = d.sigcount
                        sm = sched.csem[d.eng]
                    if key not in need or need[key][1] < v:
                        need[key] = (sm, v)
                for key, (sm, v) in need.items():
                    if known.get(key, 0) >= v:
                        continue
                    eng.wait_ge(sm, v)
                    known[key] = v
                ins = op.fn(eng)
                if ins is None:
                    continue
                if op.dma:
                    ins.then_inc(op.sem, 16)
                elif op.signal:
                    ins.then_inc(sched.csem[ename], 1)

        if self.ops["pe"]:
            @block.tensor
            def _(eng):
                run("pe", eng)
        if self.ops["act"]:
            @block.scalar
            def _(eng):
                run("act", eng)
        if self.ops["dve"]:
            @block.vector
            def _(eng):
                run("dve", eng)
        if self.ops["pool"]:
            @block.gpsimd
            def _(eng):
                run("pool", eng)
        if self.ops["sp"]:
            @block.sync
            def _(eng):
                run("sp", eng)

NT = 9
NB = 8
D = 1024
DFF = 2816
NJ = 22
EPS = 1e-6
NPP = 18 + 4 * 31


def build_nc(stages="0AB"):
    nc = bass.Bass("TRN2", target_bir_lowering=False)
    dt = nc.dram_tensor
    x_ext = dt("x_ext", [4608, D], F32, kind="ExternalInput").ap()
    w_in = dt("w_in", [D, 2560], F32, kind="ExternalInput").ap()
    w_out = dt("w_out", [D, D], F32, kind="ExternalInput").ap()
    w_gate = dt("w_gate", [D, DFF], F32, kind="ExternalInput").ap()
    w_up = dt("w_up", [D, DFF], F32, kind="ExternalInput").ap()
    w_down = dt("w_down", [DFF, D], F32, kind="ExternalInput").ap()
    g1d = dt("g1bc", [128, D], F32, kind="ExternalInput").ap()
    g2d = dt("g2bc", [128, D], F32, kind="ExternalInput").ap()
    ppd = dt("pp", [128, NPP], F32, kind="ExternalInput").ap()
    tabCd = dt("tabC", [128, 8, 640], F32, kind="ExternalInput").ap()
    tabBd = dt("tabB", [4, 128, 8, 768], F32, kind="ExternalInput").ap()
    out = dt("out", [4096, D], F32, kind="ExternalOutput").ap()
    w_in_s = dt("w_in_s", [20, 128, 1024], BF16).ap()
    w_gate_s = dt("w_gate_s", [NJ, 128, 1024], BF16).ap()
    w_up_s = dt("w_up_s", [NJ, 128, 1024], BF16).ap()
    w_out_b = dt("w_out_b", [D, D], BF16).ap()
    w_down_b = dt("w_down_b", [DFF, D], BF16).ap()
    mixT = dt("mixT", [D, 4096], BF16).ap()

    with ExitStack() as es:
        def sem(name):
            return es.enter_context(nc.semaphore(name))

        with ExitStack() as ea:
            def sb(name, shape, dtype):
                return ea.enter_context(nc.sbuf_tensor("sa_" + name, shape, dtype))

            def ps(name, shape, dtype):
                return ea.enter_context(nc.psum_tensor("pa_" + name, shape, dtype))

            def sem(name):
                return ea.enter_context(nc.semaphore(name))

            SA = {n: sem("a_" + n) for n in ("pe", "act", "dve", "pool")}
            g1bc = sb("g1bc", [128, D], F32)
            pp = sb("pp", [128, NPP], F32)
            ident = sb("ident", [128, 128], BF16)
            identf = sb("identf", [128, 128], F32)
            onesf = sb("onesf", [128, 128], F32)
            bdf = sb("bdf", [128, 128], F32)
            bd = sb("bd", [128, 128], BF16)
            diag = sb("diag", [128, 4, 31, 128], BF16)
            tabC = sb("tabC", [128, 8, 640], BF16)
            tabB = sb("tabB", [128, 8, 768], BF16)
            tabst = sb("tabst", [128, 2, 768], F32)
            wring = sb("wring", [128, 8, 1024], BF16)
            xt = sb("xt", [128, 2, D], F32)
            junk = sb("junk", [128, D], BF16)
            ss = sb("ss", [128, 2], F32)
            rstd = sb("rstd", [128, 2], F32)
            h1 = sb("h1", [128, 2, D], BF16)
            h1T = sb("h1T", [128, 8, 512], BF16)
            sig = sb("sig", [128, 2, 512], F32)
            uT = sb("uT", [128, 4, 2048], BF16)
            qraw = sb("qraw", [128, 2, 512], F32)
            qsq = sb("qsq", [128, 2, 512], BF16)
            rbc = sb("rbc", [128, 2, 512], F32)
            qT = sb("qT", [128, 4, 1536], BF16)
            kT = sb("kT", [128, 4, 1536], BF16)
            V = sb("V", [128, 12, 8, 65], BF16)
            ycv = sb("ycv", [128, 4, 256], F32)
            ysq = sb("ysq", [128, 4, 256], F32)
            mean_sb = sb("mean_sb", [128, 256], F32)
            ey2_sb = sb("ey2_sb", [128, 256], F32)
            m2 = sb("m2", [128, 256], F32)
            rstdc = sb("rstdc", [128, 256], F32)
            t1 = sb("t1", [128, 2, 256], F32)
            t2 = sb("t2", [128, 2, 256], F32)
            mc = sb("mc", [128, 2, 256], BF16)
            E = sb("E", [128, 3, 768], BF16)
            P = sb("P", [128, 3, 768], BF16)
            rden = sb("rden", [128, 2, 4], F32)
            mixa = sb("mixa", [128, 2, 512], BF16)
            mixaT = sb("mixaT", [128, 2, 4, 128], BF16)
            pz = ps("pz", [128, 2, 512], F32)
            pt = pz[:, 0, :].bitcast(BF16).rearrange("p (k m) -> p k m", m=128)
            py = ps("py", [128, 2, 512], F32)
            pst = ps("pst", [128, 512], F32)
            pS = ps("pS", [128, 1024], F32)
            pov = ps("pov", [128, 512], F32)
            pog = pov[:, 0:260].rearrange("p (h d) -> p h d", d=65)
            ptm = pov[:, 256:512].bitcast(BF16).rearrange("p (k m) -> p k m", m=128)

            sx = [sem("a_x0"), sem("a_x1")]
            swr = [sem("a_w%d" % i) for i in range(8)]
            sts = [sem("a_ts0"), sem("a_ts1")]
            smc = [sem("a_mc0"), sem("a_mc1")]
            sma = [sem("a_ma0"), sem("a_ma1")]
            sg1 = sem("a_g1")
            spp = sem("a_pp")
            swc = [sem("a_c%d" % i) for i in range(20)]
            szc = sem("a_zc")

            with (nc.Block() if "A" in stages else nullcontext()) as blk:
                sc = Sched(nc, SA)
                A = sc.add
                for c in range(20):
                    A("pool", lambda e, c=c: e.dma_start(
                        out=w_in_s[c].rearrange("p (k m) -> p k m", k=8),
                        in_=w_in[:, c * 128:(c + 1) * 128].rearrange("(k p) m -> p k m", p=128)),
                      writes=[("wins", c)], dma_sem=swc[c])
                ckeys = []

                def cast(dst, src):
                    key = ("cast", len(ckeys))
                    ckeys.append(key)
                    A("pool", lambda e, dst=dst, src=src: e.dma_start(out=dst, in_=src), writes=[key], dma_sem=szc)

                A("sp", lambda e: e.dma_start(out=g1bc[:], in_=g1d), writes=["g1bc"], dma_sem=sg1)
                A("sp", lambda e: e.dma_start(out=pp[:], in_=ppd), writes=["pp"], dma_sem=spp)
                A("pool", lambda e: e.memset(identf[:], 1.0), writes=["identf"])
                A("pool", lambda e: e.affine_select(out=identf[:], in_=identf[:], pattern=[[-1, 128]],
                                                    compare_op=ALU.is_equal, fill=0.0, base=0, channel_multiplier=1),
                  reads=["identf"], writes=["identf"])
                A("dve", lambda e: e.tensor_copy(out=ident[:], in_=identf[:]), reads=["identf"], writes=["ident"])
                A("pool", lambda e: e.memset(onesf[:], 1.0 / 512), writes=["onesf"])
                A("pool", lambda e: e.memset(bdf[:], 0.0), writes=["bdf"])
                A("pool", lambda e: e.memset(bdf[0:64, 0:64], 1.0 / 64), reads=["bdf"], writes=["bdf"])
                A("pool", lambda e: e.memset(bdf[64:128, 64:128], 1.0 / 64), reads=["bdf"], writes=["bdf"])
                A("dve", lambda e: e.tensor_copy(out=bd[:], in_=bdf[:]), reads=["bdf"], writes=["bd"])
                A("pool", lambda e: e.memset(V[:], 1.0), writes=[("V", 0), ("V", 1), ("V", 2)])

                tsn = [0]

                def load_table(dst, src, W, key):
                    for h in range(8):
                        tb = tsn[0] % 2
                        tsn[0] += 1
                        A("pool", lambda e, tb=tb, h=h: e.dma_start(out=tabst[:, tb, 0:W], in_=src[:, h, :]),
                          writes=[("tabst", tb)], dma_sem=sts[tb])
                        A("act", lambda e, tb=tb, h=h: e.activation(out=dst[:, h, 0:W], in_=tabst[:, tb, 0:W], func=AF.Exp),
                          reads=[("tabst", tb)], writes=[(key, h)])

                wn = [0]
                gn = [0]
                xn = [0]

                def wload(c):
                    slot = wn[0] % 8
                    wn[0] += 1
                    A("pool", lambda e, slot=slot, c=c: e.dma_start(out=wring[:, slot, :], in_=w_in_s[c]),
                      reads=[("wins", c)], writes=[("wr", slot)], dma_sem=swr[slot])
                    return slot

                def rstd_lnexp(buf, reads, writes, scale=1.0):
                    A("act", lambda e: e.activation(out=buf, in_=buf, func=AF.Ln, scale=scale, bias=EPS),
                      reads=reads, writes=writes)
                    A("act", lambda e: e.activation(out=buf, in_=buf, func=AF.Exp, scale=-0.5),
                      reads=writes, writes=writes)

                pending = []
                cur = [None]

                def defer(fn):
                    pending.append([2, cur[0], fn])

                def _run_pending(pred):
                    i = 0
                    while i < len(pending):
                        ent = pending[i]
                        if pred(ent):
                            pending.pop(i)
                            saved = cur[0]
                            cur[0] = ent[1]
                            ent[2]()
                            cur[0] = saved
                            i = 0
                        else:
                            i += 1

                def stage1_steps(j):
                    steps = []
                    uslot = j % 3
                    qslot = j % 3
                    hkeys = [("h1T", s) for s in range(4)]
                    st = {}

                    def a1(s):
                        xb = xn[0] % 2
                        xn[0] += 1
                        st[("xb", s)] = xb
                        tok0 = 512 * j + 128 * s
                        A("sp", lambda e, xb=xb, tok0=tok0: e.dma_start(out=xt[:, xb, :], in_=x_ext[tok0:tok0 + 128, :]),
                          writes=[("xt", xb)], dma_sem=sx[xb])

                        def post1():
                            A("act", lambda e, xb=xb: e.activation(out=junk[:], in_=xt[:, xb, :], func=AF.Square,
                                                                    accum_out=ss[:, xb:xb + 1]),
                              reads=[("xt", xb)], writes=["junk", ("ss", xb)])
                            A("act", lambda e, xb=xb: e.activation(out=rstd[:, xb:xb + 1], in_=ss[:, xb:xb + 1], func=AF.Ln,
                                                                    scale=1.0 / D, bias=EPS),
                              reads=[("ss", xb)], writes=[("rstd", xb)])
                            A("act", lambda e, xb=xb: e.activation(out=rstd[:, xb:xb + 1], in_=rstd[:, xb:xb + 1], func=AF.Exp,
                                                                    scale=-0.5),
                              reads=[("rstd", xb)], writes=[("rstd", xb)])

                            def post2():
                                A("dve", lambda e, xb=xb: e.scalar_tensor_tensor(out=h1[:, xb, :], in0=xt[:, xb, :],
                                                                                  scalar=rstd[:, xb:xb + 1], in1=g1bc[:],
                                                                                  op0=ALU.mult, op1=ALU.mult),
                                  reads=[("xt", xb), ("rstd", xb), "g1bc"], writes=[("h1", xb)])
                            defer(post2)
                        defer(post1)

                    def a2(s):
                        xb = st[("xb", s)]
                        for k in range(8):
                            A("pe", lambda e, xb=xb, k=k: e.transpose(out=pt[:, k, :], in_=h1[:, xb, k * 128:(k + 1) * 128],
                                                                       identity=ident[:]),
                              reads=[("h1", xb), "ident"], writes=[("pz", 0)])
                        defer(lambda: A("dve", lambda e, s=s: e.tensor_copy(out=h1T[:, :, s * 128:(s + 1) * 128], in_=pt),
                                        reads=[("pz", 0)], writes=[("h1T", s)]))

                    steps += [lambda: a1(0), lambda: a1(1), lambda: a2(0), lambda: a1(2), lambda: a2(1),
                              lambda: a1(3), lambda: a2(2), lambda: a2(3)]

                    def zmm(slot):
                        bk = gn[0] % 2
                        gn[0] += 1
                        for k in range(8):
                            A("pe", lambda e, bk=bk, k=k, slot=slot: e.matmul(
                                pz[:, bk, :], lhsT=wring[:, slot, k * 128:(k + 1) * 128], rhs=h1T[:, k, :],
                                start=(k == 0), stop=(k == 7)),
                              reads=hkeys + [("wr", slot)], writes=[("pz", bk)])
                        return bk

                    def gate_step(c):
                        slot = wload(4 + c)
                        bk = zmm(slot)
                        sgb = c % 2
                        sgv = sig[:, sgb, :]
                        def post():
                            A("act", lambda e, bk=bk, sgv=sgv: e.activation(out=sgv, in_=pz[:, bk, :], func=AF.Exp, scale=-1.0),
                              reads=[("pz", bk)], writes=[("sig", sgb)])
                            A("act", lambda e, sgv=sgv: e.activation(out=sgv, in_=sgv, func=AF.Ln, bias=1.0),
                              reads=[("sig", sgb)], writes=[("sig", sgb)])
                            A("act", lambda e, sgv=sgv: e.activation(out=sgv, in_=sgv, func=AF.Exp, scale=-1.0),
                              reads=[("sig", sgb)], writes=[("sig", sgb)])
                        defer(post)

                    def a_step(c):
                        sgb = c % 2
                        slot = wload(c)
                        bk = zmm(slot)
                        def post():
                            A("dve", lambda e, bk=bk, sgb=sgb, c=c: e.tensor_tensor(
                                out=uT[:, c, 512 * uslot:512 * uslot + 512], in0=pz[:, bk, :], in1=sig[:, sgb, :], op=ALU.mult),
                              reads=[("pz", bk), ("sig", sgb)], writes=[("uT", uslot, c)])
                            if uslot == 0 and j > 0:
                                A("dve", lambda e, bk=bk, sgb=sgb, c=c: e.tensor_tensor(
                                    out=uT[:, c, 1536:2048], in0=pz[:, bk, :], in1=sig[:, sgb, :], op=ALU.mult),
                                  reads=[("pz", bk), ("sig", sgb)], writes=[("uT", 3, c)])
                        defer(post)

                    steps += [lambda: gate_step(0), lambda: gate_step(1), lambda: a_step(0), lambda: gate_step(2),
                              lambda: a_step(1), lambda: gate_step(3), lambda: a_step(2), lambda: a_step(3)]

                    def qk1(c):
                        slot = wload(c)
                        bk = zmm(slot)
                        sbf = c % 2
                        def post():
                            A("dve", lambda e, bk=bk, sbf=sbf: e.tensor_copy(out=qraw[:, sbf, :], in_=pz[:, bk, :]),
                              reads=[("pz", bk)], writes=[("qraw", sbf)])
                            A("dve", lambda e, sbf=sbf: e.tensor_tensor(out=qsq[:, sbf, :], in0=qraw[:, sbf, :], in1=qraw[:, sbf, :],
                                                                         op=ALU.mult),
                              reads=[("qraw", sbf)], writes=[("qsq", sbf)])
                        defer(post)

                    def qk2(c):
                        isq = c < 12
                        hp = (c - 8) % 4
                        sbf = c % 2
                        bk2 = gn[0] % 2
                        gn[0] += 1
                        A("pe", lambda e, bk2=bk2, sbf=sbf: e.matmul(pz[:, bk2, :], lhsT=bd[:], rhs=qsq[:, sbf, :],
                                                                      start=True, stop=True),
                          reads=["bd", ("qsq", sbf)], writes=[("pz", bk2)])
                        rv = rbc[:, sbf, :]
                        dst = qT if isq else kT
                        gcol = 0 if isq else 1

                        def post():
                            A("act", lambda e, bk2=bk2, rv=rv: e.activation(out=rv, in_=pz[:, bk2, :], func=AF.Ln, bias=EPS),
                              reads=[("pz", bk2)], writes=[("rbc", sbf)])
                            A("act", lambda e, rv=rv: e.activation(out=rv, in_=rv, func=AF.Exp, scale=-0.5),
                              reads=[("rbc", sbf)], writes=[("rbc", sbf)])

                            def post2():
                                A("dve", lambda e, sbf=sbf, dst=dst, hp=hp, gcol=gcol: e.scalar_tensor_tensor(
                                    out=dst[:, hp, 512 * qslot:512 * qslot + 512], in0=qraw[:, sbf, :],
                                    scalar=pp[:, gcol:gcol + 1], in1=rbc[:, sbf, :], op0=ALU.mult, op1=ALU.mult),
                                  reads=[("qraw", sbf), ("rbc", sbf), "pp"], writes=[("qT" if isq else "kT", qslot, hp)])
                            defer(post2)
                        defer(post)

                    steps.append(lambda: qk1(8))
                    for c in range(9, 16):
                        steps.append(lambda c=c: qk1(c))
                        steps.append(lambda c=c: qk2(c - 1))
                    steps.append(lambda: qk2(15))

                    def v_step(s):
                        if s == 0:
                            vslots = [wload(16 + i) for i in range(4)]
                            s0 = vslots[0]
                            assert vslots == [s0, s0 + 1, s0 + 2, s0 + 3]
                            st["s0"] = s0
                        s0 = st["s0"]
                        bk = gn[0] % 2
                        gn[0] += 1
                        for k in range(8):
                            A("pe", lambda e, bk=bk, k=k, s=s, s0=s0: e.matmul(
                                pz[:, bk, :].rearrange("p (a b) -> p a b", b=128),
                                lhsT=h1T[:, k, s * 128:(s + 1) * 128],
                                rhs=wring[:, s0:s0 + 4, k * 128:(k + 1) * 128],
                                start=(k == 0), stop=(k == 7)),
                              reads=hkeys + [("wr", s0 + i) for i in range(4)], writes=[("pz", bk)])
                        vs = qslot * 4 + s
                        defer(lambda: A("act", lambda e, bk=bk, vs=vs: e.activation(
                            out=V[:, vs, :, 0:64], in_=pz[:, bk, :].rearrange("p (h d) -> p h d", d=64), func=AF.Copy),
                            reads=[("pz", bk)], writes=[("V", qslot)]))

                    for s in range(4):
                        steps.append(lambda s=s: v_step(s))
                    return steps

                mcn = [0]

                def conv_steps(i):
                    steps = []
                    s_a = i % 3
                    base = 512 * s_a

                    def acc(hb, c):
                        c0 = base + 256 + 256 * hb
                        pyc = py[:, c // 2, (c % 2) * 256:(c % 2) * 256 + 256]
                        for jj in range(31):
                            A("pe", lambda e, c=c, jj=jj, pyc=pyc, c0=c0: e.matmul(
                                pyc, lhsT=diag[:, c, jj, :], rhs=uT[:, c, c0 + jj - 15:c0 + jj - 15 + 256],
                                start=(jj == 0), stop=(jj == 30)),
                              reads=[("diag", c), ("uT", s_a, c), ("uT", s_a + 1, c)], writes=[("py", c // 2)])
                        if c % 2 == 1:
                            def post():
                                for cc in (c - 1, c):
                                    pycc = py[:, cc // 2, (cc % 2) * 256:(cc % 2) * 256 + 256]
                                    A("act", lambda e, cc=cc, pycc=pycc: e.activation(out=ycv[:, cc, :], in_=pycc, func=AF.Identity,
                                                                                       bias=pp[:, 2 + cc:3 + cc]),
                                      reads=[("py", cc // 2), "pp"], writes=[("ycv", cc)])

                                def post2():
                                    for cc in (c - 1, c):
                                        A("dve", lambda e, cc=cc: e.tensor_tensor(out=ysq[:, cc, :], in0=ycv[:, cc, :], in1=ycv[:, cc, :],
                                                                                   op=ALU.mult),
                                          reads=[("ycv", cc)], writes=[("ysq", cc)])
                                defer(post2)
                            defer(post)

                    def stats(hb):
                        for c in range(4):
                            A("pe", lambda e, c=c: e.matmul(pst[:, 0:256], lhsT=onesf[:], rhs=ycv[:, c, :],
                                                             start=(c == 0), stop=(c == 3)),
                              reads=["onesf", ("ycv", c)], writes=["pst"])
                        for c in range(4):
                            A("pe", lambda e, c=c: e.matmul(pst[:, 256:512], lhsT=onesf[:], rhs=ysq[:, c, :],
                                                             start=(c == 0), stop=(c == 3)),
                              reads=["onesf", ("ysq", c)], writes=["pst"])
                        def post():
                            A("act", lambda e: e.activation(out=mean_sb[:], in_=pst[:, 0:256], func=AF.Copy),
                              reads=["pst"], writes=["mean_sb"])
                            A("act", lambda e: e.activation(out=ey2_sb[:], in_=pst[:, 256:512], func=AF.Copy),
                              reads=["pst"], writes=["ey2_sb"])

                            def post2():
                                A("dve", lambda e: e.tensor_tensor(out=m2[:], in0=mean_sb[:], in1=mean_sb[:], op=ALU.mult),
                                  reads=["mean_sb"], writes=["m2"])
                                A("dve", lambda e: e.tensor_tensor(out=rstdc[:], in0=ey2_sb[:], in1=m2[:], op=ALU.subtract),
                                  reads=["ey2_sb", "m2"], writes=["rstdc"])
                                defer(lambda: rstd_lnexp(rstdc[:], ["rstdc"], ["rstdc"]))
                            defer(post2)
                        defer(post)

                    def epi(hb, c):
                        tb = mcn[0] % 2
                        mcn[0] += 1
                        t1v = t1[:, tb, :]
                        t2v = t2[:, tb, :]
                        A("dve", lambda e, c=c, t1v=t1v: e.tensor_tensor(out=t1v, in0=ycv[:, c, :], in1=mean_sb[:], op=ALU.subtract),
                          reads=[("ycv", c), "mean_sb"], writes=[("t1", tb)])
                        A("dve", lambda e, t1v=t1v: e.tensor_tensor(out=t1v, in0=t1v, in1=rstdc[:], op=ALU.mult),
                          reads=[("t1", tb), "rstdc"], writes=[("t1", tb)])
                        tok0 = 512 * i + 256 * hb

                        def post():
                            A("act", lambda e, c=c, t1v=t1v: e.activation(out=t1v, in_=t1v, func=AF.Identity,
                                                                           scale=pp[:, 6 + c:7 + c], bias=pp[:, 10 + c:11 + c]),
                              reads=[("t1", tb), "pp"], writes=[("t1", tb)])
                            A("act", lambda e, t1v=t1v, t2v=t2v: e.activation(out=t2v, in_=t1v, func=AF.Exp, scale=-1.0),
                              reads=[("t1", tb)], writes=[("t2", tb)])
                            A("act", lambda e, t2v=t2v: e.activation(out=t2v, in_=t2v, func=AF.Ln, bias=1.0),
                              reads=[("t2", tb)], writes=[("t2", tb)])
                            A("act", lambda e, t2v=t2v: e.activation(out=t2v, in_=t2v, func=AF.Exp, scale=-1.0),
                              reads=[("t2", tb)], writes=[("t2", tb)])

                            def post2():
                                A("dve", lambda e, tb=tb, t1v=t1v, t2v=t2v: e.tensor_tensor(out=mc[:, tb, :], in0=t1v, in1=t2v, op=ALU.mult),
                                  reads=[("t1", tb), ("t2", tb)], writes=[("mc", tb)])
                                A("pool", lambda e, c=c, tb=tb, tok0=tok0: e.dma_start(
                                    out=mixT[c * 128:(c + 1) * 128, tok0:tok0 + 256], in_=mc[:, tb, :]),
                                  reads=[("mc", tb)], writes=[("mixT", i)], dma_sem=smc[tb])
                            defer(post2)
                        defer(post)

                    for hb in range(2):
                        for c in range(4):
                            steps.append(lambda hb=hb, c=c: acc(hb, c))
                        steps.append(lambda hb=hb: stats(hb))
                        for c in range(4):
                            steps.append(lambda hb=hb, c=c: epi(hb, c))
                    return steps

                an = [0]
                man = [0]
                un = [0]

                def blockinfo(b):
                    if b == 0:
                        return 0, 6
                    if b == 31:
                        return 30, 6
                    return b, 5

                def attention_steps(i):
                    steps = []
                    units = [(b, h) for b in range(4 * i, 4 * i + 4) for h in range(8)]

                    def front(b, h):
                        if h == 0 and b in (0, 1, 30, 31):
                            load_table(tabB, tabBd[{0: 0, 1: 1, 30: 2, 31: 3}[b]], 768, "tabB")
                        sbuf = un[0] % 3
                        un[0] += 1
                        ck0, nch = blockinfo(b)
                        W = nch * 128
                        hp, e_ = h // 2, h % 2
                        qt_ = (b + 2) // 4
                        qcol = 512 * (qt_ % 3) + 128 * ((b + 2) % 4)
                        for cc in range(nch):
                            ck = ck0 + cc
                            ksl = (ck // 4) % 3
                            kcol = 512 * ksl + 128 * (ck % 4)
                            A("pe", lambda e, cc=cc, kcol=kcol, qcol=qcol, hp=hp, e_=e_: e.matmul(
                                pS[:, cc * 128:(cc + 1) * 128],
                                lhsT=kT[64 * e_:64 * e_ + 64, hp, kcol:kcol + 128],
                                rhs=qT[64 * e_:64 * e_ + 64, hp, qcol:qcol + 128], start=True, stop=True),
                              reads=[("kT", ksl, hp), ("qT", qt_ % 3, hp)], writes=["pS"])
                        if b in (0, 1, 30, 31):
                            tab, tkey = tabB, ("tabB", h)
                        else:
                            tab, tkey = tabC, ("tabC", h)

                        def post():
                            A("act", lambda e, sbuf=sbuf, W=W: e.activation(out=E[:, sbuf, 0:W], in_=pS[:, 0:W], func=AF.Exp,
                                                                             scale=0.125),
                              reads=["pS"], writes=[("E", sbuf)])

                            def post2():
                                A("dve", lambda e, sbuf=sbuf, W=W, tab=tab, h=h: e.tensor_tensor(
                                    out=P[:, sbuf, 0:W], in0=E[:, sbuf, 0:W], in1=tab[:, h, 0:W], op=ALU.mult),
                                  reads=[("E", sbuf), tkey], writes=[("P", sbuf)])
                            defer(post2)
                        defer(post)
                        return sbuf

                    def back(b, h, sbuf):
                        ck0, nch = blockinfo(b)
                        g, hh = h // 4, h % 4
                        for cc in range(nch):
                            ck = ck0 + cc
                            ksl = (ck // 4) % 3
                            vs = ksl * 4 + ck % 4
                            A("pe", lambda e, cc=cc, vs=vs, h=h, hh=hh, sbuf=sbuf, nch=nch: e.matmul(
                                pog[:, hh, :], lhsT=P[:, sbuf, cc * 128:(cc + 1) * 128], rhs=V[:, vs, h, :],
                                start=(cc == 0), stop=(cc == nch - 1)),
                              reads=[("P", sbuf), ("V", ksl)], writes=["pov"])
                        mb = b % 2
                        def norm():
                            rb = an[0] % 2
                            an[0] += 1
                            A("dve", lambda e, rb=rb: e.reciprocal(out=rden[:, rb, :], in_=pog[:, :, 64]),
                              reads=["pov"], writes=[("rden", rb)])
                            A("dve", lambda e, rb=rb, g=g, mb=mb: e.tensor_tensor(
                                out=mixa[:, mb, g * 256:(g + 1) * 256].rearrange("p (h d) -> p h d", d=64),
                                in0=pog[:, :, 0:64], in1=rden[:, rb, :].unsqueeze(2).to_broadcast([128, 4, 64]),
                                op=ALU.mult),
                              reads=["pov", ("rden", rb)], writes=[("mixa", mb, g)])
                            if h == 7:
                                defer(tr)

                        def tr():
                            for hp in range(4):
                                A("pe", lambda e, hp=hp, mb=mb: e.transpose(out=ptm[:, hp, :], in_=mixa[:, mb, hp * 128:(hp + 1) * 128],
                                                                             identity=ident[:]),
                                  reads=[("mixa", mb, 0), ("mixa", mb, 1), "ident"], writes=["pov"])
                            tb = man[0] % 2
                            man[0] += 1
                            tok0 = 128 * b

                            def tr2():
                                A("act", lambda e, tb=tb: e.activation(out=mixaT[:, tb, :, :], in_=ptm, func=AF.Copy),
                                  reads=["pov"], writes=[("mixaT", tb)])
                                A("pool", lambda e, tb=tb, tok0=tok0: e.dma_start(
                                    out=mixT[512:1024, tok0:tok0 + 128].rearrange("(a p) t -> p a t", p=128),
                                    in_=mixaT[:, tb, :, :]),
                                  reads=[("mixaT", tb)], writes=[("mixT", i)], dma_sem=sma[tb])
                            defer(tr2)

                        if hh == 3:
                            defer(norm)

                    st = {}

                    def step(n):
                        b, h = units[n]
                        st[n] = front(b, h)
                        if n > 1:
                            pb, ph = units[n - 2]
                            back(pb, ph, st[n - 2])

                    for n in range(len(units)):
                        steps.append(lambda n=n: step(n))
                    steps.append(lambda: back(units[-2][0], units[-2][1], st[len(units) - 2]))
                    steps.append(lambda: back(units[-1][0], units[-1][1], st[len(units) - 1]))
                    return steps

                grp = [0]

                def run_interleaved(lists):
                    grp[0] += 1
                    items = []
                    for li, L in enumerate(lists):
                        n = len(L)
                        for k, f in enumerate(L):
                            items.append(((k + 0.5) / n, li, k, f))
                    items.sort(key=lambda t: (t[0], t[1], t[2]))
                    for _, li, _, f in items:
                        sid = (grp[0], li)
                        _run_pending(lambda ent: ent[1] == sid)
                        cur[0] = sid
                        f()
                        cur[0] = None
                        for ent in pending:
                            ent[0] -= 1
                        _run_pending(lambda ent: ent[0] <= 0)
                    _run_pending(lambda ent: True)

                def consts2_steps():
                    steps = []

                    def dg(c, j0):
                        for jj in range(j0, min(j0 + 8, 31)):
                            col = 18 + c * 31 + jj
                            A("dve", lambda e, c=c, jj=jj, col=col: e.tensor_scalar(
                                out=diag[:, c, jj, :], in0=identf[:], scalar1=pp[:, col:col + 1], scalar2=None, op0=ALU.mult),
                              reads=["identf", "pp"], writes=[("diag", c)])

                    for c in range(4):
                        for j0 in range(0, 31, 8):
                            steps.append(lambda c=c, j0=j0: dg(c, j0))
                    steps.append(lambda: load_table(tabC, tabCd, 640, "tabC"))
                    return steps

                lvl = 9
                for ch in stages:
                    if ch in "1234567":
                        lvl = int(ch)
                cast_steps = []
                for k in range(8):
                    cast_steps.append(lambda k=k: cast(w_out_b[k * 128:(k + 1) * 128, :], w_out[k * 128:(k + 1) * 128, :]))
                for jf in range(NJ):
                    cast_steps.append(lambda jf=jf: cast(w_down_b[jf * 128:(jf + 1) * 128, :], w_down[jf * 128:(jf + 1) * 128, :]))
                for jf in range(NJ):
                    cast_steps.append(lambda jf=jf: cast(w_gate_s[jf].rearrange("p (k m) -> p k m", k=8),
                                                         w_gate[:, jf * 128:(jf + 1) * 128].rearrange("(k p) m -> p k m", p=128)))
                    cast_steps.append(lambda jf=jf: cast(w_up_s[jf].rearrange("p (k m) -> p k m", k=8),
                                                         w_up[:, jf * 128:(jf + 1) * 128].rearrange("(k p) m -> p k m", p=128)))
                run_interleaved([stage1_steps(0), consts2_steps()])
                run_interleaved([stage1_steps(1)])
                ngrp = NT - 2
                per = (len(cast_steps) + ngrp - 1) // ngrp
                for j in range(2, NT):
                    cs = cast_steps[(j - 2) * per:(j - 1) * per]
                    lists = [stage1_steps(j), conv_steps(j - 2), attention_steps(j - 2)]
                    if cs:
                        lists.append(cs)
                    run_interleaved(lists)
                run_interleaved([conv_steps(NT - 2), attention_steps(NT - 2)])
                A("pool", lambda e: None, reads=[("mixT", i) for i in range(NB)])
                A("pool", lambda e: None, reads=ckeys)
                if "A" in stages:
                    sc.emit(blk)

        with ExitStack() as eb:
            def sb(name, shape, dtype):
                return eb.enter_context(nc.sbuf_tensor("sb_" + name, shape, dtype))

            def ps(name, shape, dtype):
                return eb.enter_context(nc.psum_tensor("pb_" + name, shape, dtype))

            def sem(name):
                return eb.enter_context(nc.semaphore(name))

            SB = {n: sem("b_" + n) for n in ("pe", "act", "dve", "pool")}
            g2bc = sb("g2bc", [128, D], F32)
            identb = sb("identb", [128, 128], BF16)
            identfb = sb("identfb", [128, 128], F32)
            wout = sb("wout", [128, 8, D], BF16)
            wd = sb("wd", [128, NJ, D], BF16)
            wringb = sb("wringb", [128, 8, 1024], BF16)
            mixTb = sb("mixTb", [128, 8, 512], BF16)
            xtb = sb("xtb", [128, 2, D], F32)
            x1 = sb("x1", [128, 2, 4, D], F32)
            junkb = sb("junkb", [128, D], BF16)
            ssb = sb("ssb", [128, 2], F32)
            rstdb = sb("rstdb", [128, 2], F32)
            h2 = sb("h2", [128, 2, D], BF16)
            h2T = sb("h2T", [128, 2, 8, 512], BF16)
            sgl = sb("sgl", [128, 2, 512], F32)
            actT = sb("actT", [128, NJ, 512], BF16)
            ot = sb("ot", [128, 2, D], F32)
            po = ps("po", [128, 2, 512], F32)
            po2 = ps("po2", [128, 2, 512], F32)
            ptb = po2[:, 0, :].bitcast(BF16).rearrange("p (k m) -> p k m", m=128)
            pg = ps("pg", [128, 2, 512], F32)
            pu = ps("pu", [128, 2, 512], F32)

            sxb = [sem("b_x0"), sem("b_x1")]
            swb = [sem("b_w%d" % i) for i in range(8)]
            sot = [sem("b_o0"), sem("b_o1")]
            smx = sem("b_mx")
            sg2 = sem("b_g2")
            swo = sem("b_wo")
            swd = [sem("b_wd%d" % i) for i in range(2)]

            with (nc.Block() if "B" in stages else nullcontext()) as blk:
                sc = Sched(nc, SB)
                A = sc.add
                A("sp", lambda e: e.dma_start(out=g2bc[:], in_=g2d), writes=["g2bc"], dma_sem=sg2)
                A("pool", lambda e: e.dma_start(out=wout[:], in_=w_out_b.rearrange("(k p) m -> p k m", p=128)),
                  writes=["wout"], dma_sem=swo)
                A("pool", lambda e: e.memset(identfb[:], 1.0), writes=["identf"])
                A("pool", lambda e: e.affine_select(out=identfb[:], in_=identfb[:], pattern=[[-1, 128]],
                                                    compare_op=ALU.is_equal, fill=0.0, base=0, channel_multiplier=1),
                  reads=["identf"], writes=["identf"])
                A("dve", lambda e: e.tensor_copy(out=identb[:], in_=identfb[:]), reads=["identf"], writes=["ident"])

                wn = [0]
                xn = [0]
                gn = [0]
                on = [0]
                outkeys = []

                def wload(src):
                    slot = wn[0] % 8
                    wn[0] += 1
                    A("pool", lambda e, slot=slot, src=src: e.dma_start(out=wringb[:, slot, :], in_=src),
                      writes=[("wr", slot)], dma_sem=swb[slot])
                    return slot

                def prep_steps(i):
                    pb = i % 2
                    steps = []
                    st = {}

                    def p0():
                        A("sp", lambda e: e.dma_start(out=mixTb[:], in_=mixT[:, 512 * i:512 * i + 512].rearrange("(k p) t -> p k t", p=128)),
                          writes=["mixTb"], dma_sem=smx)

                    def p1(s):
                        for half in range(2):
                            for k in range(8):
                                A("pe", lambda e, s=s, half=half, k=k: e.matmul(
                                    po2[:, half, :], lhsT=mixTb[:, k, s * 128:(s + 1) * 128],
                                    rhs=wout[:, k, half * 512:(half + 1) * 512], start=(k == 0), stop=(k == 7)),
                                  reads=["mixTb", "wout"], writes=[("po2", half)])
                        xb = xn[0] % 2
                        xn[0] += 1
                        st[s] = xb
                        tok0 = 256 + 512 * i + 128 * s
                        A("sp", lambda e, xb=xb, tok0=tok0: e.dma_start(out=xtb[:, xb, :], in_=x_ext[tok0:tok0 + 128, :]),
                          writes=[("xt", xb)], dma_sem=sxb[xb])
                        A("dve", lambda e, xb=xb, s=s: e.tensor_tensor(
                            out=x1[:, pb, s, :], in0=po2[:].rearrange("p a b -> p (a b)"), in1=xtb[:, xb, :], op=ALU.add),
                          reads=[("po2", 0), ("po2", 1), ("xt", xb)], writes=[("x1", pb, s)])

                    def p2(s):
                        xb = st[s]
                        A("act", lambda e, xb=xb, s=s: e.activation(out=junkb[:], in_=x1[:, pb, s, :], func=AF.Square,
                                                                     accum_out=ssb[:, xb:xb + 1]),
                          reads=[("x1", pb, s)], writes=["junk", ("ss", xb)])
                        A("act", lambda e, xb=xb: e.activation(out=rstdb[:, xb:xb + 1], in_=ssb[:, xb:xb + 1], func=AF.Sqrt,
                                                                scale=1.0 / D, bias=EPS),
                          reads=[("ss", xb)], writes=[("rstd", xb)])
                        A("dve", lambda e, xb=xb: e.reciprocal(out=rstdb[:, xb:xb + 1], in_=rstdb[:, xb:xb + 1]),
                          reads=[("rstd", xb)], writes=[("rstd", xb)])
                        A("dve", lambda e, xb=xb, s=s: e.scalar_tensor_tensor(out=h2[:, xb, :], in0=x1[:, pb, s, :],
                                                                               scalar=rstdb[:, xb:xb + 1], in1=g2bc[:],
                                                                               op0=ALU.mult, op1=ALU.mult),
                          reads=[("x1", pb, s), ("rstd", xb), "g2bc"], writes=[("h2", xb)])

                    def p3(s):
                        xb = st[s]
                        for k in range(8):
                            A("pe", lambda e, xb=xb, k=k: e.transpose(out=ptb[:, k, :], in_=h2[:, xb, k * 128:(k + 1) * 128],
                                                                       identity=identb[:]),
                              reads=[("h2", xb), "ident"], writes=[("po2", 0)])
                        A("dve", lambda e, s=s: e.tensor_copy(out=h2T[:, pb, :, s * 128:(s + 1) * 128], in_=ptb),
                          reads=[("po2", 0)], writes=[("h2T", pb, s)])

                    steps += [p0, lambda: p1(0), lambda: p2(0), lambda: p1(1), lambda: p3(0), lambda: p2(1),
                              lambda: p1(2), lambda: p3(1), lambda: p2(2), lambda: p1(3), lambda: p3(2),
                              lambda: p2(3), lambda: p3(3)]
                    return steps

                def ffn_steps(i):
                    pb = i % 2
                    steps = []
                    hkeys = [("h2T", pb, s) for s in range(4)]
                    akeys = [("actT", jf) for jf in range(NJ)]

                    def f(jf):
                        sg_ = wload(w_gate_s[jf])
                        su_ = wload(w_up_s[jf])
                        gb = gn[0] % 2
                        gn[0] += 1
                        for k in range(8):
                            A("pe", lambda e, gb=gb, k=k, sg_=sg_: e.matmul(
                                pg[:, gb, :], lhsT=wringb[:, sg_, k * 128:(k + 1) * 128], rhs=h2T[:, pb, k, :],
                                start=(k == 0), stop=(k == 7)),
                              reads=hkeys + [("wr", sg_)], writes=[("pg", gb)])
                        for k in range(8):
                            A("pe", lambda e, gb=gb, k=k, su_=su_: e.matmul(
                                pu[:, gb, :], lhsT=wringb[:, su_, k * 128:(k + 1) * 128], rhs=h2T[:, pb, k, :],
                                start=(k == 0), stop=(k == 7)),
                              reads=hkeys + [("wr", su_)], writes=[("pu", gb)])
                        A("act", lambda e, gb=gb: e.activation(out=sgl[:, gb, :], in_=pg[:, gb, :], func=AF.Silu),
                          reads=[("pg", gb)], writes=[("sgl", gb)])
                        A("dve", lambda e, gb=gb, jf=jf: e.tensor_tensor(out=actT[:, jf, :], in0=pu[:, gb, :], in1=sgl[:, gb, :],
                                                                          op=ALU.mult),
                          reads=[("pu", gb), ("sgl", gb)], writes=[("actT", jf)])

                    def d(s):
                        for half in range(2):
                            for jf in range(NJ):
                                A("pe", lambda e, s=s, half=half, jf=jf: e.matmul(
                                    po[:, half, :], lhsT=actT[:, jf, s * 128:(s + 1) * 128],
                                    rhs=wd[:, jf, half * 512:(half + 1) * 512], start=(jf == 0), stop=(jf == NJ - 1)),
                                  reads=akeys + [("wd", 0), ("wd", 1)], writes=[("po", half)])
                        ob = on[0] % 2
                        on[0] += 1
                        A("dve", lambda e, ob=ob, s=s: e.tensor_tensor(
                            out=ot[:, ob, :], in0=po[:].rearrange("p a b -> p (a b)"), in1=x1[:, pb, s, :], op=ALU.add),
                          reads=[("po", 0), ("po", 1), ("x1", pb, s)], writes=[("ot", ob)])
                        tok0 = 512 * i + 128 * s
                        okey = ("out", i, s)
                        outkeys.append(okey)
                        A("sp", lambda e, ob=ob, tok0=tok0: e.dma_start(out=out[tok0:tok0 + 128, :], in_=ot[:, ob, :]),
                          reads=[("ot", ob)], writes=[okey], dma_sem=sot[ob])

                    for jf in range(NJ):
                        steps.append(lambda jf=jf: f(jf))
                    for s in range(4):
                        steps.append(lambda s=s: d(s))
                    return steps

                def run_interleaved(lists):
                    items = []
                    for li, L in enumerate(lists):
                        n = len(L)
                        for k, fn in enumerate(L):
                            items.append(((k + 0.5) / n, li, k, fn))
                    items.sort(key=lambda t: (t[0], t[1], t[2]))
                    for _, _, _, fn in items:
                        fn()

                run_interleaved([prep_steps(0)])
                A("pool", lambda e: e.dma_start(out=wd[:, 0:11, :], in_=w_down_b[0:1408, :].rearrange("(k p) m -> p k m", p=128)),
                  writes=[("wd", 0)], dma_sem=swd[0])
                A("pool", lambda e: e.dma_start(out=wd[:, 11:22, :], in_=w_down_b[1408:2816, :].rearrange("(k p) m -> p k m", p=128)),
                  writes=[("wd", 1)], dma_sem=swd[1])
                for i in range(NB):
                    if i + 1 < NB:
                        run_interleaved([ffn_steps(i), prep_steps(i + 1)])
                    else:
                        run_interleaved([ffn_steps(i)])
                A("sp", lambda e: None, reads=outkeys)
                if "B" in stages:
                    sc.emit(blk)
    return nc


def _bias_table(rpb, hf, b, ck0, nch, width):
    kp = np.arange(128)
    qi = np.arange(128)
    tab = np.full((128, 8, width, 128), -30000.0, dtype=np.float32)
    gr_q = 64 * hf + 2 * b + qi // 64
    cq = qi % 64
    rs = np.clip(gr_q - 4, 0, 120)
    cs = np.clip(cq - 8, 0, 48)
    for cc in range(nch):
        ck = ck0 + cc
        gr_k = 64 * hf - 4 + 2 * ck + kp // 64
        kc = kp % 64
        dr = gr_k[:, None] - gr_q[None, :]
        dc = kc[:, None] - cq[None, :]
        valid = ((gr_k[:, None] >= rs[None, :]) & (gr_k[:, None] < rs[None, :] + 8) &
                 (kc[:, None] >= cs[None, :]) & (kc[:, None] < cs[None, :] + 16))
        ri = np.clip(dr + 7, 0, 14)
        ci = np.clip(dc + 15, 0, 30)
        g = rpb[:, ri, ci]
        g = np.where(valid[None], g, np.float32(-30000.0))
        tab[:, :, cc, :] = np.transpose(g, (1, 0, 2))
    return tab.reshape(128, 8, width * 128)


_NC_CACHE = {}


def kernel(x, norm1_g, w_in, q_norm_g, k_norm_g, rpb, dw_kernel, dw_bias,
           conv_ln_g, conv_ln_b, w_out, norm2_g, w_gate, w_up, w_down):
    f32 = np.float32
    x = np.asarray(x, f32)
    rpb0 = np.asarray(rpb, f32)[0]
    g1bc = np.ascontiguousarray(np.broadcast_to(np.asarray(norm1_g, f32).reshape(1, D), (128, D)))
    g2bc = np.ascontiguousarray(np.broadcast_to(np.asarray(norm2_g, f32).reshape(1, D), (128, D)))
    pp = np.zeros((128, NPP), f32)
    pp[:, 0] = np.tile(np.asarray(q_norm_g, f32).reshape(64), 2)
    pp[:, 1] = np.tile(np.asarray(k_norm_g, f32).reshape(64), 2)
    pp[:, 2:6] = np.asarray(dw_bias, f32).reshape(4, 128).T
    pp[:, 6:10] = np.asarray(conv_ln_g, f32).reshape(4, 128).T
    pp[:, 10:14] = np.asarray(conv_ln_b, f32).reshape(4, 128).T
    dk = np.asarray(dw_kernel, f32).reshape(31, 4, 128)
    pp[:, 18:18 + 124] = np.transpose(dk, (2, 1, 0)).reshape(128, 124)
    wi = np.ascontiguousarray(np.asarray(w_in, f32).reshape(D, 2560))
    wo = np.ascontiguousarray(np.asarray(w_out, f32).reshape(D, D))
    wg = np.ascontiguousarray(np.asarray(w_gate, f32).reshape(D, DFF))
    wu = np.ascontiguousarray(np.asarray(w_up, f32).reshape(D, DFF))
    wdn = np.ascontiguousarray(np.asarray(w_down, f32).reshape(DFF, D))
    in_maps = []
    for core in range(8):
        b, hf = core // 2, core % 2
        xe = np.zeros((4608, D), f32)
        lo = hf * 4096 - 256
        hi = lo + 4608
        slo, shi = max(lo, 0), min(hi, 8192)
        xe[slo - lo:shi - lo] = x[b, slo:shi]
        tabC = _bias_table(rpb0, 0, 10, 10, 5, 5)
        tabB = np.stack([
            _bias_table(rpb0, hf, 0, 0, 6, 6),
            _bias_table(rpb0, hf, 1, 1, 5, 6),
            _bias_table(rpb0, hf, 30, 30, 5, 6),
            _bias_table(rpb0, hf, 31, 30, 6, 6),
        ])
        in_maps.append({"x_ext": xe, "w_in": wi, "w_out": wo, "w_gate": wg, "w_up": wu, "w_down": wdn,
                        "g1bc": g1bc, "g2bc": g2bc, "pp": pp, "tabC": tabC, "tabB": tabB})
    if "nc" not in _NC_CACHE:
        _NC_CACHE["nc"] = build_nc()
    nc = _NC_CACHE["nc"]
    res = run_bass_kernel_spmd(nc, in_maps, core_ids=list(range(8)))
    outp = np.empty((4, 8192, D), f32)
    for core in range(8):
        b, hf = core // 2, core % 2
        outp[b, hf * 4096:(hf + 1) * 4096] = res.results[core]["out"]
    return outp
```

```python
import numpy as np
import concourse.bass as bass
import concourse.mybir as mybir
from concourse.bass_utils import run_bass_kernel_spmd
from contextlib import ExitStack, nullcontext

F32 = mybir.dt.float32
BF16 = mybir.dt.bfloat16
AF = mybir.ActivationFunctionType
ALU = mybir.AluOpType
AX = mybir.AxisListType


class _Op:
    __slots__ = ("eng", "fn", "deps", "signal", "sigcount", "dma", "sem", "semval", "raw")

    def __init__(self, eng, fn, dma, sem):
        self.eng = eng
        self.fn = fn
        self.deps = []
        self.signal = False
        self.sigcount = 0
        self.dma = dma
        self.sem = sem
        self.semval = 0
        self.raw = set()


class Sched:
    ENGS = ("pe", "act", "dve", "pool", "sp")

    def __init__(self, nc, sems):
        self.nc = nc
        self.ops = {e: [] for e in self.ENGS}
        self.res = {}
        self.all = []
        self.csem = sems
        self.dma_cnt = {}

    def add(self, eng, fn, reads=(), writes=(), dma_sem=None):
        op = _Op(eng, fn, dma_sem is not None, dma_sem)
        deps = {}
        for r in reads:
            st = self.res.get(r)
            if st is not None and st[0] is not None:
                deps[id(st[0])] = (st[0], True)
        for w in writes:
            st = self.res.get(w)
            if st is not None:
                if st[0] is not None and id(st[0]) not in deps:
                    deps[id(st[0])] = (st[0], False)
                for rd in st[1]:
                    if id(rd) not in deps:
                        deps[id(rd)] = (rd, False)
        for r in reads:
            st = self.res.get(r)
            if st is None:
                st = [None, []]
                self.res[r] = st
            st[1].append(op)
        for w in writes:
            self.res[w] = [op, []]
        for d, israw in deps.values():
            if d is op:
                continue
            if d.dma:
                op.deps.append(d)
            elif d.eng == op.eng and not op.dma:
                if op.eng != "pe":
                    op.deps.append(d)
            elif d.eng == op.eng and op.dma:
                op.deps.append(d)
            else:
                op.deps.append(d)
        if op.dma:
            c = self.dma_cnt.get(id(dma_sem), 0) + 16
            self.dma_cnt[id(dma_sem)] = c
            op.semval = c
        self.ops[eng].append(op)
        self.all.append(op)
        return op

    def emit(self, block):
        for op in self.all:
            for d in op.deps:
                if not d.dma:
                    d.signal = True
        for e in self.ENGS:
            c = 0
            for op in self.ops[e]:
                if op.signal and not op.dma:
                    c += 1
                op.sigcount = c
        sched = self

        def run(ename, eng):
            known = {}
            for op in sched.ops[ename]:
                need = {}
                for d in op.deps:
                    if d.dma:
                        key = ("d", id(d.sem))
                        v = d.semval
                        sm = d.sem
                    else:
                        key = ("c", d.eng)
                        v = d.sigcount
                        sm = sched.csem[d.eng]
                    if key not in need or need[key][1] < v:
                        need[key] = (sm, v)
                for key, (sm, v) in need.items():
                    if known.get(key, 0) >= v:
                        continue
                    eng.wait_ge(sm, v)
                    known[key] = v
                ins = op.fn(eng)
                if ins is None:
                    continue
                if op.dma:
                    ins.then_inc(op.sem, 16)
                elif op.signal:
                    ins.then_inc(sched.csem[ename], 1)

        if self.ops["pe"]:
            @block.tensor
            def _(eng):
                run("pe", eng)
        if self.ops["act"]:
            @block.scalar
            def _(eng):
                run("act", eng)
        if self.ops["dve"]:
            @block.vector
            def _(eng):
                run("dve", eng)
        if self.ops["pool"]:
            @block.gpsimd
            def _(eng):
                run("pool", eng)
        if self.ops["sp"]:
            @block.sync
            def _(eng):
                run("sp", eng)

NT = 9
NB = 8
D = 1024
DFF = 2816
NJ = 22
EPS = 1e-6
NPP = 18 + 4 * 31


def build_nc(stages="0AB"):
    nc = bass.Bass("TRN2", target_bir_lowering=False)
    dt = nc.dram_tensor
    x_ext = dt("x_ext", [4608, D], F32, kind="ExternalInput").ap()
    w_in = dt("w_in", [D, 2560], F32, kind="ExternalInput").ap()
    w_out = dt("w_out", [D, D], F32, kind="ExternalInput").ap()
    w_gate = dt("w_gate", [D, DFF], F32, kind="ExternalInput").ap()
    w_up = dt("w_up", [D, DFF], F32, kind="ExternalInput").ap()
    w_down = dt("w_down", [DFF, D], F32, kind="ExternalInput").ap()
    g1d = dt("g1bc", [128, D], F32, kind="ExternalInput").ap()
    g2d = dt("g2bc", [128, D], F32, kind="ExternalInput").ap()
    ppd = dt("pp", [128, NPP], F32, kind="ExternalInput").ap()
    tabCd = dt("tabC", [128, 8, 640], F32, kind="ExternalInput").ap()
    tabBd = dt("tabB", [4, 128, 8, 768], F32, kind="ExternalInput").ap()
    out = dt("out", [4096, D], F32, kind="ExternalOutput").ap()
    w_in_s = dt("w_in_s", [20, 128, 1024], BF16).ap()
    w_gate_s = dt("w_gate_s", [NJ, 128, 1024], BF16).ap()
    w_up_s = dt("w_up_s", [NJ, 128, 1024], BF16).ap()
    w_out_b = dt("w_out_b", [D, D], BF16).ap()
    w_down_b = dt("w_down_b", [DFF, D], BF16).ap()
    mixT = dt("mixT", [D, 4096], BF16).ap()

    with ExitStack() as es:
        def sem(name):
            return es.enter_context(nc.semaphore(name))

        with ExitStack() as ea:
            def sb(name, shape, dtype):
                return ea.enter_context(nc.sbuf_tensor("sa_" + name, shape, dtype))

            def ps(name, shape, dtype):
                return ea.enter_context(nc.psum_tensor("pa_" + name, shape, dtype))

            def sem(name):
                return ea.enter_context(nc.semaphore(name))

            SA = {n: sem("a_" + n) for n in ("pe", "act", "dve", "pool")}
            g1bc = sb("g1bc", [128, D], F32)
            pp = sb("pp", [128, NPP], F32)
            ident = sb("ident", [128, 128], BF16)
            identf = sb("identf", [128, 128], F32)
            onesf = sb("onesf", [128, 128], F32)
            bdf = sb("bdf", [128, 128], F32)
            bd = sb("bd", [128, 128], BF16)
            diag = sb("diag", [128, 4, 31, 128], BF16)
            tabC = sb("tabC", [128, 8, 640], BF16)
            tabB = sb("tabB", [128, 8, 768], BF16)
            tabst = sb("tabst", [128, 2, 768], F32)
            wring = sb("wring", [128, 8, 1024], BF16)
            xt = sb("xt", [128, 2, D], F32)
            junk = sb("junk", [128, D], BF16)
            ss = sb("ss", [128, 2], F32)
            rstd = sb("rstd", [128, 2], F32)
            h1 = sb("h1", [128, 2, D], BF16)
            h1T = sb("h1T", [128, 8, 512], BF16)
            sig = sb("sig", [128, 2, 512], F32)
            uT = sb("uT", [128, 4, 2048], BF16)
            qraw = sb("qraw", [128, 2, 512], F32)
            qsq = sb("qsq", [128, 2, 512], BF16)
            rbc = sb("rbc", [128, 2, 512], F32)
            qT = sb("qT", [128, 4, 1536], BF16)
            kT = sb("kT", [128, 4, 1536], BF16)
            V = sb("V", [128, 12, 8, 65], BF16)
            ycv = sb("ycv", [128, 4, 256], F32)
            ysq = sb("ysq", [128, 4, 256], BF16)
            onesb = sb("onesb", [128, 128], BF16)
            mean_sb = sb("mean_sb", [128, 256], F32)
            ey2_sb = sb("ey2_sb", [128, 256], F32)
            m2 = sb("m2", [128, 256], F32)
            rstdc = sb("rstdc", [128, 256], F32)
            t1 = sb("t1", [128, 2, 256], F32)
            t2 = sb("t2", [128, 2, 256], F32)
            mc = sb("mc", [128, 2, 256], BF16)
            E = sb("E", [128, 3, 768], BF16)
            P = sb("P", [128, 3, 768], BF16)
            rden = sb("rden", [128, 2, 4], F32)
            mixa = sb("mixa", [128, 2, 512], BF16)
            mixaT = sb("mixaT", [128, 2, 4, 128], BF16)
            pz = ps("pz", [128, 2, 512], F32)
            pt = pz[:, 0, :].bitcast(BF16).rearrange("p (k m) -> p k m", m=128)
            py = ps("py", [128, 2, 512], F32)
            pst = ps("pst", [128, 512], F32)
            pS = ps("pS", [128, 1024], F32)
            pov = ps("pov", [128, 512], F32)
            pog = pov[:, 0:260].rearrange("p (h d) -> p h d", d=65)
            ptm = pov[:, 256:512].bitcast(BF16).rearrange("p (k m) -> p k m", m=128)

            sx = [sem("a_x0"), sem("a_x1")]
            swr = [sem("a_w%d" % i) for i in range(8)]
            sts = [sem("a_ts0"), sem("a_ts1")]
            smc = [sem("a_mc0"), sem("a_mc1")]
            sma = [sem("a_ma0"), sem("a_ma1")]
            sg1 = sem("a_g1")
            spp = sem("a_pp")
            swc = [sem("a_c%d" % i) for i in range(20)]
            szc = sem("a_zc")

            with (nc.Block() if "A" in stages else nullcontext()) as blk:
                sc = Sched(nc, SA)
                A = sc.add
                for c in range(20):
                    A("pool", lambda e, c=c: e.dma_start(
                        out=w_in_s[c].rearrange("p (k m) -> p k m", k=8),
                        in_=w_in[:, c * 128:(c + 1) * 128].rearrange("(k p) m -> p k m", p=128)),
                      writes=[("wins", c)], dma_sem=swc[c])
                ckeys = []

                def cast(dst, src):
                    key = ("cast", len(ckeys))
                    ckeys.append(key)
                    A("pool", lambda e, dst=dst, src=src: e.dma_start(out=dst, in_=src), writes=[key], dma_sem=szc)

                A("sp", lambda e: e.dma_start(out=g1bc[:], in_=g1d), writes=["g1bc"], dma_sem=sg1)
                A("sp", lambda e: e.dma_start(out=pp[:], in_=ppd), writes=["pp"], dma_sem=spp)
                A("pool", lambda e: e.memset(identf[:], 1.0), writes=["identf"])
                A("pool", lambda e: e.affine_select(out=identf[:], in_=identf[:], pattern=[[-1, 128]],
                                                    compare_op=ALU.is_equal, fill=0.0, base=0, channel_multiplier=1),
                  reads=["identf"], writes=["identf"])
                A("dve", lambda e: e.tensor_copy(out=ident[:], in_=identf[:]), reads=["identf"], writes=["ident"])
                A("pool", lambda e: e.memset(onesf[:], 1.0 / 512), writes=["onesf"])
                A("dve", lambda e: e.tensor_copy(out=onesb[:], in_=onesf[:]), reads=["onesf"], writes=["onesb"])
                A("pool", lambda e: e.memset(bdf[:], 0.0), writes=["bdf"])
                A("pool", lambda e: e.memset(bdf[0:64, 0:64], 1.0 / 64), reads=["bdf"], writes=["bdf"])
                A("pool", lambda e: e.memset(bdf[64:128, 64:128], 1.0 / 64), reads=["bdf"], writes=["bdf"])
                A("dve", lambda e: e.tensor_copy(out=bd[:], in_=bdf[:]), reads=["bdf"], writes=["bd"])
                A("pool", lambda e: e.memset(V[:], 1.0), writes=[("V", 0), ("V", 1), ("V", 2)])

                tsn = [0]

                def load_table(dst, src, W, key):
                    for h in range(8):
                        tb = tsn[0] % 2
                        tsn[0] += 1
                        A("pool", lambda e, tb=tb, h=h: e.dma_start(out=tabst[:, tb, 0:W], in_=src[:, h, :]),
                          writes=[("tabst", tb)], dma_sem=sts[tb])
                        A("act", lambda e, tb=tb, h=h: e.activation(out=dst[:, h, 0:W], in_=tabst[:, tb, 0:W], func=AF.Exp),
                          reads=[("tabst", tb)], writes=[(key, h)])

                wn = [0]
                gn = [0]
                xn = [0]

                def wload(c):
                    slot = wn[0] % 8
                    wn[0] += 1
                    A("pool", lambda e, slot=slot, c=c: e.dma_start(out=wring[:, slot, :], in_=w_in_s[c]),
                      reads=[("wins", c)], writes=[("wr", slot)], dma_sem=swr[slot])
                    return slot

                def rstd_lnexp(buf, reads, writes, scale=1.0):
                    A("act", lambda e: e.activation(out=buf, in_=buf, func=AF.Ln, scale=scale, bias=EPS),
                      reads=reads, writes=writes)
                    A("act", lambda e: e.activation(out=buf, in_=buf, func=AF.Exp, scale=-0.5),
                      reads=writes, writes=writes)

                pending = []
                cur = [None]

                def defer(fn):
                    pending.append([2, cur[0], fn])

                def _run_pending(pred):
                    i = 0
                    while i < len(pending):
                        ent = pending[i]
                        if pred(ent):
                            pending.pop(i)
                            saved = cur[0]
                            cur[0] = ent[1]
                            ent[2]()
                            cur[0] = saved
                            i = 0
                        else:
                            i += 1

                def stage1_steps(j):
                    steps = []
                    uslot = j % 3
                    qslot = j % 3
                    hkeys = [("h1T", s) for s in range(4)]
                    st = {}

                    def a1(s):
                        xb = xn[0] % 2
                        xn[0] += 1
                        st[("xb", s)] = xb
                        tok0 = 512 * j + 128 * s
                        A("sp", lambda e, xb=xb, tok0=tok0: e.dma_start(out=xt[:, xb, :], in_=x_ext[tok0:tok0 + 128, :]),
                          writes=[("xt", xb)], dma_sem=sx[xb])

                        def post1():
                            A("act", lambda e, xb=xb: e.activation(out=junk[:], in_=xt[:, xb, :], func=AF.Square,
                                                                    accum_out=ss[:, xb:xb + 1]),
                              reads=[("xt", xb)], writes=["junk", ("ss", xb)])
                            A("act", lambda e, xb=xb: e.activation(out=rstd[:, xb:xb + 1], in_=ss[:, xb:xb + 1], func=AF.Ln,
                                                                    scale=1.0 / D, bias=EPS),
                              reads=[("ss", xb)], writes=[("rstd", xb)])
                            A("act", lambda e, xb=xb: e.activation(out=rstd[:, xb:xb + 1], in_=rstd[:, xb:xb + 1], func=AF.Exp,
                                                                    scale=-0.5),
                              reads=[("rstd", xb)], writes=[("rstd", xb)])

                            def post2():
                                A("dve", lambda e, xb=xb: e.scalar_tensor_tensor(out=h1[:, xb, :], in0=xt[:, xb, :],
                                                                                  scalar=rstd[:, xb:xb + 1], in1=g1bc[:],
                                                                                  op0=ALU.mult, op1=ALU.mult),
                                  reads=[("xt", xb), ("rstd", xb), "g1bc"], writes=[("h1", xb)])
                            defer(post2)
                        defer(post1)

                    def a2(s):
                        xb = st[("xb", s)]
                        for k in range(8):
                            A("pe", lambda e, xb=xb, k=k: e.transpose(out=pt[:, k, :], in_=h1[:, xb, k * 128:(k + 1) * 128],
                                                                       identity=ident[:]),
                              reads=[("h1", xb), "ident"], writes=[("pz", 0)])
                        defer(lambda: A("dve", lambda e, s=s: e.tensor_copy(out=h1T[:, :, s * 128:(s + 1) * 128], in_=pt),
                                        reads=[("pz", 0)], writes=[("h1T", s)]))

                    steps += [lambda: a1(0), lambda: a1(1), lambda: a2(0), lambda: a1(2), lambda: a2(1),
                              lambda: a1(3), lambda: a2(2), lambda: a2(3)]

                    def zmm(slot):
                        bk = gn[0] % 2
                        gn[0] += 1
                        for k in range(8):
                            A("pe", lambda e, bk=bk, k=k, slot=slot: e.matmul(
                                pz[:, bk, :], lhsT=wring[:, slot, k * 128:(k + 1) * 128], rhs=h1T[:, k, :],
                                start=(k == 0), stop=(k == 7)),
                              reads=hkeys + [("wr", slot)], writes=[("pz", bk)])
                        return bk

                    def gate_step(c):
                        slot = wload(4 + c)
                        bk = zmm(slot)
                        sgb = c % 2
                        sgv = sig[:, sgb, :]
                        def post():
                            A("act", lambda e, bk=bk, sgv=sgv: e.activation(out=sgv, in_=pz[:, bk, :], func=AF.Exp, scale=-1.0),
                              reads=[("pz", bk)], writes=[("sig", sgb)])
                            A("act", lambda e, sgv=sgv: e.activation(out=sgv, in_=sgv, func=AF.Ln, bias=1.0),
                              reads=[("sig", sgb)], writes=[("sig", sgb)])
                            A("act", lambda e, sgv=sgv: e.activation(out=sgv, in_=sgv, func=AF.Exp, scale=-1.0),
                              reads=[("sig", sgb)], writes=[("sig", sgb)])
                        defer(post)

                    def a_step(c):
                        sgb = c % 2
                        slot = wload(c)
                        bk = zmm(slot)
                        def post():
                            A("dve", lambda e, bk=bk, sgb=sgb, c=c: e.tensor_tensor(
                                out=uT[:, c, 512 * uslot:512 * uslot + 512], in0=pz[:, bk, :], in1=sig[:, sgb, :], op=ALU.mult),
                              reads=[("pz", bk), ("sig", sgb)], writes=[("uT", uslot, c)])
                            if uslot == 0 and j > 0:
                                A("dve", lambda e, bk=bk, sgb=sgb, c=c: e.tensor_tensor(
                                    out=uT[:, c, 1536:2048], in0=pz[:, bk, :], in1=sig[:, sgb, :], op=ALU.mult),
                                  reads=[("pz", bk), ("sig", sgb)], writes=[("uT", 3, c)])
                        defer(post)

                    steps += [lambda: gate_step(0), lambda: gate_step(1), lambda: a_step(0), lambda: gate_step(2),
                              lambda: a_step(1), lambda: gate_step(3), lambda: a_step(2), lambda: a_step(3)]

                    def qk1(c):
                        slot = wload(c)
                        bk = zmm(slot)
                        sbf = c % 2
                        def post():
                            A("dve", lambda e, bk=bk, sbf=sbf: e.tensor_copy(out=qraw[:, sbf, :], in_=pz[:, bk, :]),
                              reads=[("pz", bk)], writes=[("qraw", sbf)])
                            A("dve", lambda e, sbf=sbf: e.tensor_tensor(out=qsq[:, sbf, :], in0=qraw[:, sbf, :], in1=qraw[:, sbf, :],
                                                                         op=ALU.mult),
                              reads=[("qraw", sbf)], writes=[("qsq", sbf)])
                        defer(post)

                    def qk2(c):
                        isq = c < 12
                        hp = (c - 8) % 4
                        sbf = c % 2
                        bk2 = gn[0] % 2
                        gn[0] += 1
                        A("pe", lambda e, bk2=bk2, sbf=sbf: e.matmul(pz[:, bk2, :], lhsT=bd[:], rhs=qsq[:, sbf, :],
                                                                      start=True, stop=True),
                          reads=["bd", ("qsq", sbf)], writes=[("pz", bk2)])
                        rv = rbc[:, sbf, :]
                        dst = qT if isq else kT
                        gcol = 0 if isq else 1

                        def post():
                            A("act", lambda e, bk2=bk2, rv=rv: e.activation(out=rv, in_=pz[:, bk2, :], func=AF.Ln, bias=EPS),
                              reads=[("pz", bk2)], writes=[("rbc", sbf)])
                            A("act", lambda e, rv=rv: e.activation(out=rv, in_=rv, func=AF.Exp, scale=-0.5),
                              reads=[("rbc", sbf)], writes=[("rbc", sbf)])

                            def post2():
                                A("dve", lambda e, sbf=sbf, dst=dst, hp=hp, gcol=gcol: e.scalar_tensor_tensor(
                                    out=dst[:, hp, 512 * qslot:512 * qslot + 512], in0=qraw[:, sbf, :],
                                    scalar=pp[:, gcol:gcol + 1], in1=rbc[:, sbf, :], op0=ALU.mult, op1=ALU.mult),
                                  reads=[("qraw", sbf), ("rbc", sbf), "pp"], writes=[("qT" if isq else "kT", qslot, hp)])
                            defer(post2)
                        defer(post)

                    steps.append(lambda: qk1(8))
                    for c in range(9, 16):
                        steps.append(lambda c=c: qk1(c))
                        steps.append(lambda c=c: qk2(c - 1))
                    steps.append(lambda: qk2(15))

                    def v_step(s):
                        if s == 0:
                            vslots = [wload(16 + i) for i in range(4)]
                            s0 = vslots[0]
                            assert vslots == [s0, s0 + 1, s0 + 2, s0 + 3]
                            st["s0"] = s0
                        s0 = st["s0"]
                        bk = gn[0] % 2
                        gn[0] += 1
                        for k in range(8):
                            A("pe", lambda e, bk=bk, k=k, s=s, s0=s0: e.matmul(
                                pz[:, bk, :].rearrange("p (a b) -> p a b", b=128),
                                lhsT=h1T[:, k, s * 128:(s + 1) * 128],
                                rhs=wring[:, s0:s0 + 4, k * 128:(k + 1) * 128],
                                start=(k == 0), stop=(k == 7)),
                              reads=hkeys + [("wr", s0 + i) for i in range(4)], writes=[("pz", bk)])
                        vs = qslot * 4 + s
                        defer(lambda: A("act", lambda e, bk=bk, vs=vs: e.activation(
                            out=V[:, vs, :, 0:64], in_=pz[:, bk, :].rearrange("p (h d) -> p h d", d=64), func=AF.Copy),
                            reads=[("pz", bk)], writes=[("V", qslot)]))

                    for s in range(4):
                        steps.append(lambda s=s: v_step(s))
                    return steps

                mcn = [0]

                def conv_steps(i):
                    steps = []
                    s_a = i % 3
                    base = 512 * s_a

                    def acc(hb, c):
                        c0 = base + 256 + 256 * hb
                        pyc = py[:, c // 2, (c % 2) * 256:(c % 2) * 256 + 256]
                        for jj in range(31):
                            A("pe", lambda e, c=c, jj=jj, pyc=pyc, c0=c0: e.matmul(
                                pyc, lhsT=diag[:, c, jj, :], rhs=uT[:, c, c0 + jj - 15:c0 + jj - 15 + 256],
                                start=(jj == 0), stop=(jj == 30)),
                              reads=[("diag", c), ("uT", s_a, c), ("uT", s_a + 1, c)], writes=[("py", c // 2)])
                        if c % 2 == 1:
                            def post():
                                for cc in (c - 1, c):
                                    pycc = py[:, cc // 2, (cc % 2) * 256:(cc % 2) * 256 + 256]
                                    A("act", lambda e, cc=cc, pycc=pycc: e.activation(out=ycv[:, cc, :], in_=pycc, func=AF.Identity,
                                                                                       bias=pp[:, 2 + cc:3 + cc]),
                                      reads=[("py", cc // 2), "pp"], writes=[("ycv", cc)])

                                def post2():
                                    for cc in (c - 1, c):
                                        A("dve", lambda e, cc=cc: e.tensor_tensor(out=ysq[:, cc, :], in0=ycv[:, cc, :], in1=ycv[:, cc, :],
                                                                                   op=ALU.mult),
                                          reads=[("ycv", cc)], writes=[("ysq", cc)])
                                defer(post2)
                            defer(post)

                    def stats(hb):
                        for c in range(4):
                            A("pe", lambda e, c=c: e.matmul(pst[:, 0:256], lhsT=onesf[:], rhs=ycv[:, c, :],
                                                             start=(c == 0), stop=(c == 3)),
                              reads=["onesf", ("ycv", c)], writes=["pst"])
                        for c in range(4):
                            A("pe", lambda e, c=c: e.matmul(pst[:, 256:512], lhsT=onesb[:], rhs=ysq[:, c, :],
                                                             start=(c == 0), stop=(c == 3)),
                              reads=["onesb", ("ysq", c)], writes=["pst"])
                        def post():
                            A("act", lambda e: e.activation(out=mean_sb[:], in_=pst[:, 0:256], func=AF.Copy),
                              reads=["pst"], writes=["mean_sb"])
                            A("act", lambda e: e.activation(out=ey2_sb[:], in_=pst[:, 256:512], func=AF.Copy),
                              reads=["pst"], writes=["ey2_sb"])

                            def post2():
                                A("dve", lambda e: e.tensor_tensor(out=m2[:], in0=mean_sb[:], in1=mean_sb[:], op=ALU.mult),
                                  reads=["mean_sb"], writes=["m2"])
                                A("dve", lambda e: e.tensor_tensor(out=rstdc[:], in0=ey2_sb[:], in1=m2[:], op=ALU.subtract),
                                  reads=["ey2_sb", "m2"], writes=["rstdc"])
                                defer(lambda: rstd_lnexp(rstdc[:], ["rstdc"], ["rstdc"]))
                            defer(post2)
                        defer(post)

                    def epi(hb, c):
                        tb = mcn[0] % 2
                        mcn[0] += 1
                        t1v = t1[:, tb, :]
                        t2v = t2[:, tb, :]
                        A("dve", lambda e, c=c, t1v=t1v: e.tensor_tensor(out=t1v, in0=ycv[:, c, :], in1=mean_sb[:], op=ALU.subtract),
                          reads=[("ycv", c), "mean_sb"], writes=[("t1", tb)])
                        A("dve", lambda e, t1v=t1v: e.tensor_tensor(out=t1v, in0=t1v, in1=rstdc[:], op=ALU.mult),
                          reads=[("t1", tb), "rstdc"], writes=[("t1", tb)])
                        tok0 = 512 * i + 256 * hb

                        def post():
                            A("act", lambda e, c=c, t1v=t1v: e.activation(out=t1v, in_=t1v, func=AF.Identity,
                                                                           scale=pp[:, 6 + c:7 + c], bias=pp[:, 10 + c:11 + c]),
                              reads=[("t1", tb), "pp"], writes=[("t1", tb)])
                            A("act", lambda e, t1v=t1v, t2v=t2v: e.activation(out=t2v, in_=t1v, func=AF.Exp, scale=-1.0),
                              reads=[("t1", tb)], writes=[("t2", tb)])
                            A("act", lambda e, t2v=t2v: e.activation(out=t2v, in_=t2v, func=AF.Ln, bias=1.0),
                              reads=[("t2", tb)], writes=[("t2", tb)])
                            A("act", lambda e, t2v=t2v: e.activation(out=t2v, in_=t2v, func=AF.Exp, scale=-1.0),
                              reads=[("t2", tb)], writes=[("t2", tb)])

                            def post2():
                                A("dve", lambda e, tb=tb, t1v=t1v, t2v=t2v: e.tensor_tensor(out=mc[:, tb, :], in0=t1v, in1=t2v, op=ALU.mult),
                                  reads=[("t1", tb), ("t2", tb)], writes=[("mc", tb)])
                                A("sp", lambda e, c=c, tb=tb, tok0=tok0: e.dma_start(
                                    out=mixT[c * 128:(c + 1) * 128, tok0:tok0 + 256], in_=mc[:, tb, :]),
                                  reads=[("mc", tb)], writes=[("mixT", i)], dma_sem=smc[tb])
                            defer(post2)
                        defer(post)

                    for hb in range(2):
                        for c in range(4):
                            steps.append(lambda hb=hb, c=c: acc(hb, c))
                        steps.append(lambda hb=hb: stats(hb))
                        for c in range(4):
                            steps.append(lambda hb=hb, c=c: epi(hb, c))
                    return steps

                an = [0]
                man = [0]
                un = [0]

                def blockinfo(b):
                    if b == 0:
                        return 0, 6
                    if b == 31:
                        return 30, 6
                    return b, 5

                def attention_steps(i):
                    steps = []
                    units = [(b, h) for b in range(4 * i, 4 * i + 4) for h in range(8)]

                    def front(b, h):
                        if h == 0 and b in (0, 1, 30, 31):
                            load_table(tabB, tabBd[{0: 0, 1: 1, 30: 2, 31: 3}[b]], 768, "tabB")
                        sbuf = un[0] % 3
                        un[0] += 1
                        ck0, nch = blockinfo(b)
                        W = nch * 128
                        hp, e_ = h // 2, h % 2
                        qt_ = (b + 2) // 4
                        qcol = 512 * (qt_ % 3) + 128 * ((b + 2) % 4)
                        for cc in range(nch):
                            ck = ck0 + cc
                            ksl = (ck // 4) % 3
                            kcol = 512 * ksl + 128 * (ck % 4)
                            A("pe", lambda e, cc=cc, kcol=kcol, qcol=qcol, hp=hp, e_=e_: e.matmul(
                                pS[:, cc * 128:(cc + 1) * 128],
                                lhsT=kT[64 * e_:64 * e_ + 64, hp, kcol:kcol + 128],
                                rhs=qT[64 * e_:64 * e_ + 64, hp, qcol:qcol + 128], start=True, stop=True),
                              reads=[("kT", ksl, hp), ("qT", qt_ % 3, hp)], writes=["pS"])
                        if b in (0, 1, 30, 31):
                            tab, tkey = tabB, ("tabB", h)
                        else:
                            tab, tkey = tabC, ("tabC", h)

                        def post():
                            A("act", lambda e, sbuf=sbuf, W=W: e.activation(out=E[:, sbuf, 0:W], in_=pS[:, 0:W], func=AF.Exp,
                                                                             scale=0.125),
                              reads=["pS"], writes=[("E", sbuf)])

                            def post2():
                                A("dve", lambda e, sbuf=sbuf, W=W, tab=tab, h=h: e.tensor_tensor(
                                    out=P[:, sbuf, 0:W], in0=E[:, sbuf, 0:W], in1=tab[:, h, 0:W], op=ALU.mult),
                                  reads=[("E", sbuf), tkey], writes=[("P", sbuf)])
                            defer(post2)
                        defer(post)
                        return sbuf

                    def back(b, h, sbuf):
                        ck0, nch = blockinfo(b)
                        g, hh = h // 4, h % 4
                        for cc in range(nch):
                            ck = ck0 + cc
                            ksl = (ck // 4) % 3
                            vs = ksl * 4 + ck % 4
                            A("pe", lambda e, cc=cc, vs=vs, h=h, hh=hh, sbuf=sbuf, nch=nch: e.matmul(
                                pog[:, hh, :], lhsT=P[:, sbuf, cc * 128:(cc + 1) * 128], rhs=V[:, vs, h, :],
                                start=(cc == 0), stop=(cc == nch - 1)),
                              reads=[("P", sbuf), ("V", ksl)], writes=["pov"])
                        mb = b % 2
                        def norm():
                            rb = an[0] % 2
                            an[0] += 1
                            A("dve", lambda e, rb=rb: e.reciprocal(out=rden[:, rb, :], in_=pog[:, :, 64]),
                              reads=["pov"], writes=[("rden", rb)])
                            A("dve", lambda e, rb=rb, g=g, mb=mb: e.tensor_tensor(
                                out=mixa[:, mb, g * 256:(g + 1) * 256].rearrange("p (h d) -> p h d", d=64),
                                in0=pog[:, :, 0:64], in1=rden[:, rb, :].unsqueeze(2).to_broadcast([128, 4, 64]),
                                op=ALU.mult),
                              reads=["pov", ("rden", rb)], writes=[("mixa", mb, g)])
                            if h == 7:
                                defer(tr)

                        def tr():
                            for hp in range(4):
                                A("pe", lambda e, hp=hp, mb=mb: e.transpose(out=ptm[:, hp, :], in_=mixa[:, mb, hp * 128:(hp + 1) * 128],
                                                                             identity=ident[:]),
                                  reads=[("mixa", mb, 0), ("mixa", mb, 1), "ident"], writes=["pov"])
                            tb = man[0] % 2
                            man[0] += 1
                            tok0 = 128 * b

                            def tr2():
                                A("act", lambda e, tb=tb: e.activation(out=mixaT[:, tb, :, :], in_=ptm, func=AF.Copy),
                                  reads=["pov"], writes=[("mixaT", tb)])
                                A("sp", lambda e, tb=tb, tok0=tok0: e.dma_start(
                                    out=mixT[512:1024, tok0:tok0 + 128].rearrange("(a p) t -> p a t", p=128),
                                    in_=mixaT[:, tb, :, :]),
                                  reads=[("mixaT", tb)], writes=[("mixT", i)], dma_sem=sma[tb])
                            defer(tr2)

                        if hh == 3:
                            defer(norm)

                    st = {}

                    def step(n):
                        b, h = units[n]
                        st[n] = front(b, h)
                        if n > 1:
                            pb, ph = units[n - 2]
                            back(pb, ph, st[n - 2])

                    for n in range(len(units)):
                        steps.append(lambda n=n: step(n))
                    steps.append(lambda: back(units[-2][0], units[-2][1], st[len(units) - 2]))
                    steps.append(lambda: back(units[-1][0], units[-1][1], st[len(units) - 1]))
                    return steps

                grp = [0]

                def run_interleaved(lists):
                    grp[0] += 1
                    items = []
                    for li, L in enumerate(lists):
                        n = len(L)
                        for k, f in enumerate(L):
                            items.append(((k + 0.5) / n, li, k, f))
                    items.sort(key=lambda t: (t[0], t[1], t[2]))
                    for _, li, _, f in items:
                        sid = (grp[0], li)
                        _run_pending(lambda ent: ent[1] == sid)
                        cur[0] = sid
                        f()
                        cur[0] = None
                        for ent in pending:
                            ent[0] -= 1
                        _run_pending(lambda ent: ent[0] <= 0)
                    _run_pending(lambda ent: True)

                def consts2_steps():
                    steps = []

                    def dg(c, j0):
                        for jj in range(j0, min(j0 + 8, 31)):
                            col = 18 + c * 31 + jj
                            A("dve", lambda e, c=c, jj=jj, col=col: e.tensor_scalar(
                                out=diag[:, c, jj, :], in0=identf[:], scalar1=pp[:, col:col + 1], scalar2=None, op0=ALU.mult),
                              reads=["identf", "pp"], writes=[("diag", c)])

                    for c in range(4):
                        for j0 in range(0, 31, 8):
                            steps.append(lambda c=c, j0=j0: dg(c, j0))
                    steps.append(lambda: load_table(tabC, tabCd, 640, "tabC"))
                    return steps

                lvl = 9
                for ch in stages:
                    if ch in "1234567":
                        lvl = int(ch)
                cast_steps = []
                for k in range(8):
                    cast_steps.append(lambda k=k: cast(w_out_b[k * 128:(k + 1) * 128, :], w_out[k * 128:(k + 1) * 128, :]))
                for jf in range(NJ):
                    cast_steps.append(lambda jf=jf: cast(w_down_b[jf * 128:(jf + 1) * 128, :], w_down[jf * 128:(jf + 1) * 128, :]))
                for jf in range(NJ):
                    cast_steps.append(lambda jf=jf: cast(w_gate_s[jf].rearrange("p (k m) -> p k m", k=8),
                                                         w_gate[:, jf * 128:(jf + 1) * 128].rearrange("(k p) m -> p k m", p=128)))
                    cast_steps.append(lambda jf=jf: cast(w_up_s[jf].rearrange("p (k m) -> p k m", k=8),
                                                         w_up[:, jf * 128:(jf + 1) * 128].rearrange("(k p) m -> p k m", p=128)))
                run_interleaved([stage1_steps(0), consts2_steps()])
                run_interleaved([stage1_steps(1)])
                ngrp = NT - 2
                per = (len(cast_steps) + ngrp - 1) // ngrp
                for j in range(2, NT):
                    cs = cast_steps[(j - 2) * per:(j - 1) * per]
                    lists = [stage1_steps(j), conv_steps(j - 2), attention_steps(j - 2)]
                    if cs:
                        lists.append(cs)
                    run_interleaved(lists)
                run_interleaved([conv_steps(NT - 2), attention_steps(NT - 2)])
                A("sp", lambda e: None, reads=[("mixT", i) for i in range(NB)])
                A("pool", lambda e: None, reads=ckeys)
                if "A" in stages:
                    sc.emit(blk)

        with ExitStack() as eb:
            def sb(name, shape, dtype):
                return eb.enter_context(nc.sbuf_tensor("sb_" + name, shape, dtype))

            def ps(name, shape, dtype):
                return eb.enter_context(nc.psum_tensor("pb_" + name, shape, dtype))

            def sem(name):
                return eb.enter_context(nc.semaphore(name))

            SB = {n: sem("b_" + n) for n in ("pe", "act", "dve", "pool")}
            g2bc = sb("g2bc", [128, D], F32)
            identb = sb("identb", [128, 128], BF16)
            identfb = sb("identfb", [128, 128], F32)
            wout = sb("wout", [128, 8, D], BF16)
            wd = sb("wd", [128, NJ, D], BF16)
            wringb = sb("wringb", [128, 8, 1024], BF16)
            mixTb = sb("mixTb", [128, 8, 512], BF16)
            xtb = sb("xtb", [128, 2, D], F32)
            x1 = sb("x1", [128, 2, 4, D], F32)
            junkb = sb("junkb", [128, D], BF16)
            ssb = sb("ssb", [128, 2], F32)
            rstdb = sb("rstdb", [128, 2], F32)
            h2 = sb("h2", [128, 2, D], BF16)
            h2T = sb("h2T", [128, 2, 8, 512], BF16)
            sgl = sb("sgl", [128, 2, 512], F32)
            actT = sb("actT", [128, NJ, 512], BF16)
            ot = sb("ot", [128, 2, D], F32)
            po = ps("po", [128, 2, 512], F32)
            po2 = ps("po2", [128, 2, 512], F32)
            ptb = po2[:, 0, :].bitcast(BF16).rearrange("p (k m) -> p k m", m=128)
            pg = ps("pg", [128, 2, 512], F32)
            pu = ps("pu", [128, 2, 512], F32)

            sxb = [sem("b_x0"), sem("b_x1")]
            swb = [sem("b_w%d" % i) for i in range(8)]
            sot = [sem("b_o0"), sem("b_o1")]
            smx = sem("b_mx")
            sg2 = sem("b_g2")
            swo = sem("b_wo")
            swd = [sem("b_wd%d" % i) for i in range(2)]

            with (nc.Block() if "B" in stages else nullcontext()) as blk:
                sc = Sched(nc, SB)
                A = sc.add
                A("sp", lambda e: e.dma_start(out=g2bc[:], in_=g2d), writes=["g2bc"], dma_sem=sg2)
                A("pool", lambda e: e.dma_start(out=wout[:], in_=w_out_b.rearrange("(k p) m -> p k m", p=128)),
                  writes=["wout"], dma_sem=swo)
                A("pool", lambda e: e.memset(identfb[:], 1.0), writes=["identf"])
                A("pool", lambda e: e.affine_select(out=identfb[:], in_=identfb[:], pattern=[[-1, 128]],
                                                    compare_op=ALU.is_equal, fill=0.0, base=0, channel_multiplier=1),
                  reads=["identf"], writes=["identf"])
                A("dve", lambda e: e.tensor_copy(out=identb[:], in_=identfb[:]), reads=["identf"], writes=["ident"])

                wn = [0]
                xn = [0]
                gn = [0]
                on = [0]
                outkeys = []

                def wload(src):
                    slot = wn[0] % 8
                    wn[0] += 1
                    A("pool", lambda e, slot=slot, src=src: e.dma_start(out=wringb[:, slot, :], in_=src),
                      writes=[("wr", slot)], dma_sem=swb[slot])
                    return slot

                def prep_steps(i):
                    pb = i % 2
                    steps = []
                    st = {}

                    def p0():
                        A("sp", lambda e: e.dma_start(out=mixTb[:], in_=mixT[:, 512 * i:512 * i + 512].rearrange("(k p) t -> p k t", p=128)),
                          writes=["mixTb"], dma_sem=smx)

                    def p1(s):
                        for half in range(2):
                            for k in range(8):
                                A("pe", lambda e, s=s, half=half, k=k: e.matmul(
                                    po2[:, half, :], lhsT=mixTb[:, k, s * 128:(s + 1) * 128],
                                    rhs=wout[:, k, half * 512:(half + 1) * 512], start=(k == 0), stop=(k == 7)),
                                  reads=["mixTb", "wout"], writes=[("po2", half)])
                        xb = xn[0] % 2
                        xn[0] += 1
                        st[s] = xb
                        tok0 = 256 + 512 * i + 128 * s
                        A("sp", lambda e, xb=xb, tok0=tok0: e.dma_start(out=xtb[:, xb, :], in_=x_ext[tok0:tok0 + 128, :]),
                          writes=[("xt", xb)], dma_sem=sxb[xb])
                        A("dve", lambda e, xb=xb, s=s: e.tensor_tensor(
                            out=x1[:, pb, s, :], in0=po2[:].rearrange("p a b -> p (a b)"), in1=xtb[:, xb, :], op=ALU.add),
                          reads=[("po2", 0), ("po2", 1), ("xt", xb)], writes=[("x1", pb, s)])

                    def p2(s):
                        xb = st[s]
                        A("act", lambda e, xb=xb, s=s: e.activation(out=junkb[:], in_=x1[:, pb, s, :], func=AF.Square,
                                                                     accum_out=ssb[:, xb:xb + 1]),
                          reads=[("x1", pb, s)], writes=["junk", ("ss", xb)])
                        A("act", lambda e, xb=xb: e.activation(out=rstdb[:, xb:xb + 1], in_=ssb[:, xb:xb + 1], func=AF.Sqrt,
                                                                scale=1.0 / D, bias=EPS),
                          reads=[("ss", xb)], writes=[("rstd", xb)])
                        A("dve", lambda e, xb=xb: e.reciprocal(out=rstdb[:, xb:xb + 1], in_=rstdb[:, xb:xb + 1]),
                          reads=[("rstd", xb)], writes=[("rstd", xb)])
                        A("dve", lambda e, xb=xb, s=s: e.scalar_tensor_tensor(out=h2[:, xb, :], in0=x1[:, pb, s, :],
                                                                               scalar=rstdb[:, xb:xb + 1], in1=g2bc[:],
                                                                               op0=ALU.mult, op1=ALU.mult),
                          reads=[("x1", pb, s), ("rstd", xb), "g2bc"], writes=[("h2", xb)])

                    def p3(s):
                        xb = st[s]
                        for k in range(8):
                            A("pe", lambda e, xb=xb, k=k: e.transpose(out=ptb[:, k, :], in_=h2[:, xb, k * 128:(k + 1) * 128],
                                                                       identity=identb[:]),
                              reads=[("h2", xb), "ident"], writes=[("po2", 0)])
                        A("dve", lambda e, s=s: e.tensor_copy(out=h2T[:, pb, :, s * 128:(s + 1) * 128], in_=ptb),
                          reads=[("po2", 0)], writes=[("h2T", pb, s)])

                    steps += [p0, lambda: p1(0), lambda: p2(0), lambda: p1(1), lambda: p3(0), lambda: p2(1),
                              lambda: p1(2), lambda: p3(1), lambda: p2(2), lambda: p1(3), lambda: p3(2),
                              lambda: p2(3), lambda: p3(3)]
                    return steps

                def ffn_steps(i):
                    pb = i % 2
                    steps = []
                    hkeys = [("h2T", pb, s) for s in range(4)]
                    akeys = [("actT", jf) for jf in range(NJ)]

                    def f(jf):
                        sg_ = wload(w_gate_s[jf])
                        su_ = wload(w_up_s[jf])
                        gb = gn[0] % 2
                        gn[0] += 1
                        for k in range(8):
                            A("pe", lambda e, gb=gb, k=k, sg_=sg_: e.matmul(
                                pg[:, gb, :], lhsT=wringb[:, sg_, k * 128:(k + 1) * 128], rhs=h2T[:, pb, k, :],
                                start=(k == 0), stop=(k == 7)),
                              reads=hkeys + [("wr", sg_)], writes=[("pg", gb)])
                        for k in range(8):
                            A("pe", lambda e, gb=gb, k=k, su_=su_: e.matmul(
                                pu[:, gb, :], lhsT=wringb[:, su_, k * 128:(k + 1) * 128], rhs=h2T[:, pb, k, :],
                                start=(k == 0), stop=(k == 7)),
                              reads=hkeys + [("wr", su_)], writes=[("pu", gb)])
                        A("act", lambda e, gb=gb: e.activation(out=sgl[:, gb, :], in_=pg[:, gb, :], func=AF.Silu),
                          reads=[("pg", gb)], writes=[("sgl", gb)])
                        A("dve", lambda e, gb=gb, jf=jf: e.tensor_tensor(out=actT[:, jf, :], in0=pu[:, gb, :], in1=sgl[:, gb, :],
                                                                          op=ALU.mult),
                          reads=[("pu", gb), ("sgl", gb)], writes=[("actT", jf)])

                    def d(s):
                        for half in range(2):
                            for jf in range(NJ):
                                A("pe", lambda e, s=s, half=half, jf=jf: e.matmul(
                                    po[:, half, :], lhsT=actT[:, jf, s * 128:(s + 1) * 128],
                                    rhs=wd[:, jf, half * 512:(half + 1) * 512], start=(jf == 0), stop=(jf == NJ - 1)),
                                  reads=akeys + [("wd", 0), ("wd", 1)], writes=[("po", half)])
                        ob = on[0] % 2
                        on[0] += 1
                        A("dve", lambda e, ob=ob, s=s: e.tensor_tensor(
                            out=ot[:, ob, :], in0=po[:].rearrange("p a b -> p (a b)"), in1=x1[:, pb, s, :], op=ALU.add),
                          reads=[("po", 0), ("po", 1), ("x1", pb, s)], writes=[("ot", ob)])
                        tok0 = 512 * i + 128 * s
                        okey = ("out", i, s)
                        outkeys.append(okey)
                        A("sp", lambda e, ob=ob, tok0=tok0: e.dma_start(out=out[tok0:tok0 + 128, :], in_=ot[:, ob, :]),
                          reads=[("ot", ob)], writes=[okey], dma_sem=sot[ob])

                    for jf in range(NJ):
                        steps.append(lambda jf=jf: f(jf))
                    for s in range(4):
                        steps.append(lambda s=s: d(s))
                    return steps

                def run_interleaved(lists):
                    items = []
                    for li, L in enumerate(lists):
                        n = len(L)
                        for k, fn in enumerate(L):
                            items.append(((k + 0.5) / n, li, k, fn))
                    items.sort(key=lambda t: (t[0], t[1], t[2]))
                    for _, _, _, fn in items:
                        fn()

                run_interleaved([prep_steps(0)])
                A("pool", lambda e: e.dma_start(out=wd[:, 0:11, :], in_=w_down_b[0:1408, :].rearrange("(k p) m -> p k m", p=128)),
                  writes=[("wd", 0)], dma_sem=swd[0])
                A("pool", lambda e: e.dma_start(out=wd[:, 11:22, :], in_=w_down_b[1408:2816, :].rearrange("(k p) m -> p k m", p=128)),
                  writes=[("wd", 1)], dma_sem=swd[1])
                for i in range(NB):
                    if i + 1 < NB:
                        run_interleaved([ffn_steps(i), prep_steps(i + 1)])
                    else:
                        run_interleaved([ffn_steps(i)])
                A("sp", lambda e: None, reads=outkeys)
                if "B" in stages:
                    sc.emit(blk)
    return nc


def _bias_table(rpb, hf, b, ck0, nch, width):
    kp = np.arange(128)
    qi = np.arange(128)
    tab = np.full((128, 8, width, 128), -30000.0, dtype=np.float32)
    gr_q = 64 * hf + 2 * b + qi // 64
    cq = qi % 64
    rs = np.clip(gr_q - 4, 0, 120)
    cs = np.clip(cq - 8, 0, 48)
    for cc in range(nch):
        ck = ck0 + cc
        gr_k = 64 * hf - 4 + 2 * ck + kp // 64
        kc = kp % 64
        dr = gr_k[:, None] - gr_q[None, :]
        dc = kc[:, None] - cq[None, :]
        valid = ((gr_k[:, None] >= rs[None, :]) & (gr_k[:, None] < rs[None, :] + 8) &
                 (kc[:, None] >= cs[None, :]) & (kc[:, None] < cs[None, :] + 16))
        ri = np.clip(dr + 7, 0, 14)
        ci = np.clip(dc + 15, 0, 30)
        g = rpb[:, ri, ci]
        g = np.where(valid[None], g, np.float32(-30000.0))
        tab[:, :, cc, :] = np.transpose(g, (1, 0, 2))
    return tab.reshape(128, 8, width * 128)


_NC_CACHE = {}


def kernel(x, norm1_g, w_in, q_norm_g, k_norm_g, rpb, dw_kernel, dw_bias,
           conv_ln_g, conv_ln_b, w_out, norm2_g, w_gate, w_up, w_down):
    f32 = np.float32
    x = np.asarray(x, f32)
    rpb0 = np.asarray(rpb, f32)[0]
    g1bc = np.ascontiguousarray(np.broadcast_to(np.asarray(norm1_g, f32).reshape(1, D), (128, D)))
    g2bc = np.ascontiguousarray(np.broadcast_to(np.asarray(norm2_g, f32).reshape(1, D), (128, D)))
    pp = np.zeros((128, NPP), f32)
    pp[:, 0] = np.tile(np.asarray(q_norm_g, f32).reshape(64), 2)
    pp[:, 1] = np.tile(np.asarray(k_norm_g, f32).reshape(64), 2)
    pp[:, 2:6] = np.asarray(dw_bias, f32).reshape(4, 128).T
    pp[:, 6:10] = np.asarray(conv_ln_g, f32).reshape(4, 128).T
    pp[:, 10:14] = np.asarray(conv_ln_b, f32).reshape(4, 128).T
    dk = np.asarray(dw_kernel, f32).reshape(31, 4, 128)
    pp[:, 18:18 + 124] = np.transpose(dk, (2, 1, 0)).reshape(128, 124)
    wi = np.ascontiguousarray(np.asarray(w_in, f32).reshape(D, 2560))
    wo = np.ascontiguousarray(np.asarray(w_out, f32).reshape(D, D))
    wg = np.ascontiguousarray(np.asarray(w_gate, f32).reshape(D, DFF))
    wu = np.ascontiguousarray(np.asarray(w_up, f32).reshape(D, DFF))
    wdn = np.ascontiguousarray(np.asarray(w_down, f32).reshape(DFF, D))
    in_maps = []
    for core in range(8):
        b, hf = core // 2, core % 2
        xe = np.zeros((4608, D), f32)
        lo = hf * 4096 - 256
        hi = lo + 4608
        slo, shi = max(lo, 0), min(hi, 8192)
        xe[slo - lo:shi - lo] = x[b, slo:shi]
        tabC = _bias_table(rpb0, 0, 10, 10, 5, 5)
        tabB = np.stack([
            _bias_table(rpb0, hf, 0, 0, 6, 6),
            _bias_table(rpb0, hf, 1, 1, 5, 6),
            _bias_table(rpb0, hf, 30, 30, 5, 6),
            _bias_table(rpb0, hf, 31, 30, 6, 6),
        ])
        in_maps.append({"x_ext": xe, "w_in": wi, "w_out": wo, "w_gate": wg, "w_up": wu, "w_down": wdn,
                        "g1bc": g1bc, "g2bc": g2bc, "pp": pp, "tabC": tabC, "tabB": tabB})
    if "nc" not in _NC_CACHE:
        _NC_CACHE["nc"] = build_nc()
    nc = _NC_CACHE["nc"]
    res = run_bass_kernel_spmd(nc, in_maps, core_ids=list(range(8)))
    outp = np.empty((4, 8192, D), f32)
    for core in range(8):
        b, hf = core // 2, core % 2
        outp[b, hf * 4096:(hf + 1) * 4096] = res.results[core]["out"]
    return outp
```
